# Optimizing a Trainium2 kernel written in Bass

```python
import jax, jax.numpy as jnp
from jax import lax
import numpy as np


D_MODEL = 1024
BATCH = 16
SEQ = 2048
DEPTH = 4

GRID_W = 64
HEAD_DIM = 64
N_Q_HEADS = 8
N_KV_HEADS = 2
Q_PER_KV = N_Q_HEADS // N_KV_HEADS
ATTN_WIDTH = N_Q_HEADS * HEAD_DIM
KV_WIDTH = N_KV_HEADS * HEAD_DIM
ROPE_THETA = 10000.0
Q_BLOCK = 128
CONV_WIDTH = D_MODEL // 2
POOL_WINDOWS = (2, 4, 8, 16)
N_POOL_GROUPS = 4
POOL_WIDTH = D_MODEL // 2
POOL_GROUP = POOL_WIDTH // N_POOL_GROUPS
SGU_WIDTH = D_MODEL // 2
N_SGU_GROUPS = 4
SGU_GROUP = SGU_WIDTH // N_SGU_GROUPS
SGU_CHUNK = 128
D_FF = 2816
EPS = 1e-6
EVEN_IN = 3 * CONV_WIDTH + ATTN_WIDTH + 2 * KV_WIDTH
EVEN_SPLITS = (CONV_WIDTH, 2 * CONV_WIDTH, 3 * CONV_WIDTH,
               3 * CONV_WIDTH + ATTN_WIDTH, 3 * CONV_WIDTH + ATTN_WIDTH + KV_WIDTH)
EVEN_MIX = CONV_WIDTH + ATTN_WIDTH
ODD_IN = POOL_WIDTH + 2 * SGU_WIDTH
ODD_SPLITS = (POOL_WIDTH, POOL_WIDTH + SGU_WIDTH)
ODD_MIX = POOL_WIDTH + SGU_WIDTH

kernel_name = 'hybrid_conv_gqa_pool_sgu_macaron_encoder'


def rms_norm(x, g):
    xf = x.astype(jnp.float32)
    y = xf * lax.rsqrt(jnp.mean(xf * xf, axis=-1, keepdims=True) + EPS)
    return (y * g.astype(jnp.float32)).astype(x.dtype)


def swiglu(x, w_in, w_out):
    g, u = jnp.split(x @ w_in, 2, axis=-1)
    return (jax.nn.silu(g) * u) @ w_out


def axial_rope_tables(seq):
    rows = seq // GRID_W
    r_idx, c_idx = jnp.meshgrid(jnp.arange(rows), jnp.arange(GRID_W), indexing='ij')
    r_idx = r_idx.reshape(-1).astype(jnp.float32)
    c_idx = c_idx.reshape(-1).astype(jnp.float32)
    n_freq = HEAD_DIM // 4
    inv = ROPE_THETA ** (-jnp.arange(n_freq, dtype=jnp.float32) / n_freq)
    ang = jnp.concatenate([r_idx[:, None] * inv, c_idx[:, None] * inv], axis=-1)
    return jnp.cos(ang), jnp.sin(ang)


def apply_rope(x, cos, sin):
    b, s, h, d = x.shape
    xf = x.astype(jnp.float32).reshape(b, s, h, d // 2, 2)
    x0, x1 = xf[..., 0], xf[..., 1]
    c = cos[None, :, None, :]
    sn = sin[None, :, None, :]
    out = jnp.stack([x0 * c - x1 * sn, x0 * sn + x1 * c], axis=-1)
    return out.reshape(b, s, h, d).astype(x.dtype)


def blocked_gqa(q, k, v):
    b, s, _, _ = q.shape
    nblk = s // Q_BLOCK
    qb = q.reshape(b, nblk, Q_BLOCK, N_KV_HEADS, Q_PER_KV, HEAD_DIM).transpose(1, 0, 2, 3, 4, 5)
    scale = HEAD_DIM ** -0.5

    def one_block(qi):
        sc = jnp.einsum('bqkgd,bskd->bkgqs', qi, k).astype(jnp.float32) * scale
        p = jax.nn.softmax(sc, axis=-1).astype(v.dtype)
        return jnp.einsum('bkgqs,bskd->bqkgd', p, v)

    o = lax.map(one_block, qb)
    return o.transpose(1, 0, 2, 3, 4, 5).reshape(b, s, ATTN_WIDTH)


def short_conv(h, w):
    hp = jnp.pad(h, ((0, 0), (1, 1), (0, 0)))
    return hp[:, :-2] * w[0] + hp[:, 1:-1] * w[1] + hp[:, 2:] * w[2]


def even_mixer(h, w_in, conv_w, q_g, k_g, w_out, cos, sin):
    b, s, _ = h.shape
    gate_b, gate_c, hc, q, k, v = jnp.split(h @ w_in, list(EVEN_SPLITS), axis=-1)
    a_out = gate_b * short_conv(gate_c * hc, conv_w)
    q = apply_rope(rms_norm(q.reshape(b, s, N_Q_HEADS, HEAD_DIM), q_g), cos, sin)
    k = apply_rope(rms_norm(k.reshape(b, s, N_KV_HEADS, HEAD_DIM), k_g), cos, sin)
    v = v.reshape(b, s, N_KV_HEADS, HEAD_DIM)
    b_out = blocked_gqa(q, k, v)
    return jnp.concatenate([a_out, b_out], axis=-1) @ w_out


def multiscale_pool(p):
    b, s, _ = p.shape
    pf = p.astype(jnp.float32)
    cs = jnp.concatenate([jnp.zeros((b, 1, POOL_WIDTH), jnp.float32), jnp.cumsum(pf, axis=1)], axis=1)
    t = jnp.arange(s)
    outs = []
    for gi, w in enumerate(POOL_WINDOWS):
        r = w // 2
        lo = jnp.maximum(t - r, 0)
        hi = jnp.minimum(t + r, s - 1)
        sl = slice(gi * POOL_GROUP, (gi + 1) * POOL_GROUP)
        csg = cs[:, :, sl]
        win = csg[:, hi + 1] - csg[:, lo]
        cnt = (hi - lo + 1).astype(jnp.float32)[None, :, None]
        outs.append(win / cnt - pf[:, :, sl])
    return jnp.concatenate(outs, axis=-1).astype(p.dtype)


def chunked_sgu(u, v, norm_g, w_s, b_s):
    b, s, _ = u.shape
    v = rms_norm(v, norm_g)
    n = s // SGU_CHUNK
    vc = v.reshape(b, n, SGU_CHUNK, N_SGU_GROUPS, SGU_GROUP)
    mixed = jnp.einsum('gpq,bnqgc->bnpgc', w_s, vc) + b_s.T[None, None, :, :, None]
    return u * mixed.reshape(b, s, SGU_WIDTH)


def odd_mixer(h, w_in, pool_w, pool_scale, sgu_norm, sgu_w, sgu_b, w_out):
    b, s, _ = h.shape
    p, u, v = jnp.split(h @ w_in, list(ODD_SPLITS), axis=-1)
    pooled = multiscale_pool(p).reshape(b, s, N_POOL_GROUPS, POOL_GROUP)
    c_out = jnp.einsum('bsgc,gcd->bsgd', pooled, pool_w).reshape(b, s, POOL_WIDTH) * pool_scale
    d_out = chunked_sgu(jax.nn.gelu(u), jax.nn.gelu(v), sgu_norm, sgu_w, sgu_b)
    return jnp.concatenate([c_out, d_out], axis=-1) @ w_out


def setup_inputs(seed: int = 0) -> dict:
    key = jax.random.key(seed)
    ks = jax.random.split(key, 24)
    n_even = (DEPTH + 1) // 2
    n_odd = DEPTH // 2
    f32 = jnp.float32

    def nrm(k, shape, scale):
        return jax.random.normal(k, shape, f32) * scale

    def gain(k, shape):
        return 1.0 + 0.02 * jax.random.normal(k, shape, f32)

    return {
        'x': jax.random.normal(ks[0], (BATCH, SEQ, D_MODEL), f32),
        'ffn1_norm': gain(ks[1], (DEPTH, D_MODEL)),
        'ffn1_w_in': nrm(ks[2], (DEPTH, D_MODEL, 2 * D_FF), D_MODEL ** -0.5),
        'ffn1_w_out': nrm(ks[3], (DEPTH, D_FF, D_MODEL), D_FF ** -0.5),
        'mix_norm': gain(ks[4], (DEPTH, D_MODEL)),
        'ffn2_norm': gain(ks[5], (DEPTH, D_MODEL)),
        'ffn2_w_in': nrm(ks[6], (DEPTH, D_MODEL, 2 * D_FF), D_MODEL ** -0.5),
        'ffn2_w_out': nrm(ks[7], (DEPTH, D_FF, D_MODEL), D_FF ** -0.5),
        'ev_w_in': nrm(ks[8], (n_even, D_MODEL, EVEN_IN), D_MODEL ** -0.5),
        'ev_conv_w': nrm(ks[9], (n_even, 3, CONV_WIDTH), 3 ** -0.5),
        'ev_q_norm': gain(ks[10], (n_even, HEAD_DIM)),
        'ev_k_norm': gain(ks[11], (n_even, HEAD_DIM)),
        'ev_w_out': nrm(ks[12], (n_even, EVEN_MIX, D_MODEL), EVEN_MIX ** -0.5),
        'od_w_in': nrm(ks[13], (n_odd, D_MODEL, ODD_IN), D_MODEL ** -0.5),
        'od_pool_w': nrm(ks[14], (n_odd, N_POOL_GROUPS, POOL_GROUP, POOL_GROUP), POOL_GROUP ** -0.5),
        'od_pool_scale': 1.0 + 0.1 * jax.random.normal(ks[15], (n_odd, POOL_WIDTH), f32),
        'od_sgu_norm': gain(ks[16], (n_odd, SGU_WIDTH)),
        'od_sgu_w': nrm(ks[17], (n_odd, N_SGU_GROUPS, SGU_CHUNK, SGU_CHUNK), SGU_CHUNK ** -0.5),
        'od_sgu_b': 1.0 + 0.01 * jax.random.normal(ks[18], (n_odd, N_SGU_GROUPS, SGU_CHUNK), f32),
        'od_w_out': nrm(ks[19], (n_odd, ODD_MIX, D_MODEL), ODD_MIX ** -0.5),
        'final_norm': gain(ks[20], (D_MODEL,)),
    }


def reference(x, ffn1_norm, ffn1_w_in, ffn1_w_out, mix_norm, ffn2_norm, ffn2_w_in, ffn2_w_out,
              ev_w_in, ev_conv_w, ev_q_norm, ev_k_norm, ev_w_out,
              od_w_in, od_pool_w, od_pool_scale, od_sgu_norm, od_sgu_w, od_sgu_b, od_w_out,
              final_norm):
    s = x.shape[1]
    cos, sin = axial_rope_tables(s)
    for layer in range(DEPTH):
        x = x + 0.5 * swiglu(rms_norm(x, ffn1_norm[layer]), ffn1_w_in[layer], ffn1_w_out[layer])
        h = rms_norm(x, mix_norm[layer])
        j = layer // 2
        if layer % 2 == 0:
            x = x + even_mixer(h, ev_w_in[j], ev_conv_w[j], ev_q_norm[j], ev_k_norm[j], ev_w_out[j], cos, sin)
        else:
            x = x + odd_mixer(h, od_w_in[j], od_pool_w[j], od_pool_scale[j], od_sgu_norm[j],
                              od_sgu_w[j], od_sgu_b[j], od_w_out[j])
        x = x + 0.5 * swiglu(rms_norm(x, ffn2_norm[layer]), ffn2_w_in[layer], ffn2_w_out[layer])
    return rms_norm(x, final_norm)
```

```python
import numpy as np
from contextlib import ExitStack
import concourse.bass as bass
import concourse.mybir as mybir
from concourse.bass_utils import run_bass_kernel_spmd

F32 = mybir.dt.float32
BF16 = mybir.dt.bfloat16
ALU = mybir.AluOpType
AF = mybir.ActivationFunctionType
AX = mybir.AxisListType

D = 1024
SEQ = 2048
DFF = 2816
NJ = 22
TW = 512
NT = SEQ // TW
PAD = 8
HW = SEQ + 2 * PAD
EPS = 1e-6
N_CORES = 8
EV_COLS = 3200
THIRDS = [(0, 8), (8, 8), (16, 6)]

def PC_NORM(l, w):
    return (l * 3 + w) * 8
PC_CONV = 96
PC_QK = 120
PC_PSC = 128
NPAR = 136

EPOCH = 4096
KDMA = 8


class Op:
    __slots__ = ("eng", "fn", "is_dma", "deps", "signal", "cidx", "qidx")

    def __init__(self, eng, fn, is_dma):
        self.eng = eng
        self.fn = fn
        self.is_dma = is_dma
        self.deps = []
        self.signal = False
        self.cidx = -1
        self.qidx = -1


class Buf:
    __slots__ = ("name", "w", "r")

    def __init__(self, name):
        self.name = name
        self.w = None
        self.r = {}


class Sched:
    ENGS = ("pe", "act", "dve", "pool", "sp")

    def __init__(self):
        self.ops = {e: [] for e in self.ENGS}
        self.nops = 0
        self._bar_pos = {}

    def add(self, eng, fn, reads=(), writes=(), dma=False):
        op = Op(eng, fn, dma)
        deps = {}
        for b in reads:
            if b.w is not None:
                deps[id(b.w)] = b.w
        for b in writes:
            if b.w is not None:
                deps[id(b.w)] = b.w
            for o in b.r.values():
                deps[id(o)] = o
        for d in deps.values():
            if d.eng == "pe" and eng == "pe" and not d.is_dma and not dma:
                continue
            op.deps.append(d)
            d.signal = True
        for b in reads:
            key = (eng, len(b.r)) if dma else eng
            b.r[key] = op
        for b in writes:
            b.w = op
            b.r = {}
        self.ops[eng].append(op)
        self.nops += 1
        return op

    def barrier(self, engs=None):
        lasts = []
        for eng in self.ENGS:
            ops = self.ops[eng]
            start = self._bar_pos.get(eng, 0)
            last_c = None
            for op in ops[start:]:
                if op.fn is None:
                    continue
                if op.is_dma:
                    lasts.append(op)
                else:
                    last_c = op
            if last_c is None:
                for op in reversed(ops[:start]):
                    if op.fn is not None and not op.is_dma:
                        last_c = op
                        break
            if last_c is not None:
                lasts.append(last_c)
        for eng in self.ENGS:
            self._bar_pos[eng] = len(self.ops[eng])
        for eng in (engs or self.ENGS):
            op = Op(eng, None, False)
            op.deps = list(lasts)
            for d in lasts:
                d.signal = True
            self.ops[eng].append(op)

    def emit(self, nc, stack, block):
        csems = {}
        dsems = {}
        for eng in self.ENGS:
            c = 0
            q = 0
            for op in self.ops[eng]:
                if op.is_dma:
                    op.qidx = q
                    q += 1
                elif op.signal:
                    op.cidx = c
                    c += 1
            csems[eng] = [stack.enter_context(nc.semaphore(f"c_{eng}_{i}")) for i in range((c + EPOCH - 1) // EPOCH)]
            dsems[eng] = [stack.enter_context(nc.semaphore(f"d_{eng}_{i}")) for i in range(min(q, KDMA))]

        def sig(op):
            if op.is_dma:
                return dsems[op.eng][op.qidx % KDMA], 16 * (op.qidx // KDMA + 1)
            return csems[op.eng][op.cidx // EPOCH], op.cidx % EPOCH + 1

        def run(engname, e):
            seen = {}

            def wait(s, v):
                k = id(s)
                if seen.get(k, 0) < v:
                    e.wait_ge(s, v)
                    seen[k] = v

            for op in self.ops[engname]:
                for d in op.deps:
                    wait(*sig(d))
                if op.is_dma:
                    s, v = sig(op)
                    if v > 16:
                        wait(s, v - 16)
                if op.fn is None:
                    continue
                inst = op.fn(e)
                if op.is_dma:
                    inst.then_inc(s, 16)
                elif op.signal:
                    s, v = sig(op)
                    inst.then_inc(s, 1)

        block.tensor(lambda e: run("pe", e))
        block.scalar(lambda e: run("act", e))
        block.vector(lambda e: run("dve", e))
        block.gpsimd(lambda e: run("pool", e))
        block.sync(lambda e: run("sp", e))


def default_cfg():
    return dict(nseq=2, layers=[0, 1, 2, 3], phases=("ffn1", "mix", "ffn2"))


def build_program(cfg):
    nseq = cfg["nseq"]
    layers = cfg["layers"]
    phases = cfg["phases"]
    ntok = nseq * SEQ

    nc = bass.Bass("TRN2", target_bir_lowering=False)
    dr = {}

    def din(name, shape):
        dr[name] = nc.dram_tensor(name, list(shape), F32, kind="ExternalInput").ap()
        return dr[name]

    x_d = din("x", [ntok, D])
    out_d = nc.dram_tensor("out", [ntok, D], F32, kind="ExternalOutput").ap()
    w_in_d = {1: din("ffn1_w_in", [4, D, 2 * DFF]), 2: din("ffn2_w_in", [4, D, 2 * DFF])}
    w_out_d = {1: din("ffn1_w_out", [4, DFF, D]), 2: din("ffn2_w_out", [4, DFF, D])}
    ev_w_in_d = din("ev_w_in", [2, D, EV_COLS])
    ev_w_out_d = din("ev_w_out", [2, D, D])
    od_w_in_d = din("od_w_in", [2, D, 1536])
    od_w_out_d = din("od_w_out", [2, D, D])
    pool_w_d = din("od_pool_w", [2, 128, 4, 128])
    sgu_wt_d = din("od_sgu_wt", [2, 128, 4, 128])
    par_d = din("params", [128, NPAR])
    rope_d = din("rope", [128, 2, SEQ])
    fin_g_d = din("fin_g", [128, D])
    sgu_g_d = din("sgu_g", [2, 128, 512])
    sgu_b_d = din("sgu_b", [2, 128, 4, 512])
    edge_d = din("pool_edge", [128, 4, 16])
    cst_d = din("consts", [128, 3, 128])

    S = Sched()
    stack = ExitStack()
    with stack:
        def sb(name, shape, dt):
            return stack.enter_context(nc.sbuf_tensor(name, shape, dt))

        x_sb = sb("x_sb", [128, 8, SEQ], F32)
        ident = sb("ident", [128, 128], F32)
        cbf = sb("cbf", [128, 2, 128], BF16)
        par = sb("par", [128, NPAR], F32)
        slotA = sb("slotA", [128, 3, 8, 512], BF16)
        regB = sb("regB", [128, 8, 1024], BF16)
        ARENA_BYTES = 102784
        arena = sb("arena", [128, ARENA_BYTES // 4], F32)
        psum = stack.enter_context(nc.psum_tensor("psum", [128, 8, 512], F32))

        ones_bf = cbf[:, 0, :]
        blk_bf = cbf[:, 1, :]

        def carve(off, nbytes, dt):
            assert off % 4 == 0 and nbytes % 4 == 0 and off + nbytes <= ARENA_BYTES, (off, nbytes)
            v = arena[:, off // 4:(off + nbytes) // 4]
            if dt != F32:
                v = v.bitcast(dt)
            return v

        off = 0
        h_sb = carve(off, 8 * HW * 2, BF16).rearrange("p (k t) -> p k t", k=8)
        off += 8 * HW * 2
        sq_sb = [carve(off + i * 1024, 1024, BF16) for i in range(4)]
        off += 4096
        sd_sb = [carve(off + i * 2048, 2048, F32) for i in range(2)]
        off += 4096
        rstd_sb = [carve(off + i * 2048, 2048, F32) for i in range(2)]
        off += 4096
        PH0 = off

        xB = [[Buf(f"x{m}_{t}") for t in range(NT)] for m in range(8)]
        hB = [[Buf(f"h{k}_{t}") for t in range(NT)] for k in range(8)]
        hpadB = Buf("hpad")
        sqB = [Buf(f"sq{i}") for i in range(4)]
        sdB = [Buf(f"sd{i}") for i in range(2)]
        rstdB = [Buf(f"rstd{i}") for i in range(2)]
        bankB = [Buf(f"bank{i}") for i in range(8)]
        slotB = [Buf(f"slot{i}") for i in range(3)]
        regBB = Buf("regB")
        cstB = Buf("cst")
        parB = Buf("par")
        ctr = {"sq": 0, "sd": 0, "slot": 0}

        def bank(i):
            return psum[:, i, :]

        S.add("sp", lambda e: e.dma_start(out=ident[:], in_=cst_d[:, 0, :]), writes=[cstB], dma=True)
        cst2B = Buf("cst2")
        S.add("pool", lambda e: e.dma_start(out=cbf[:], in_=cst_d[:, 1:3, :]), writes=[cst2B], dma=True)
        S.add("sp", lambda e: e.dma_start(out=par[:], in_=par_d), writes=[parB], dma=True)

        def pcol(c):
            return par[:, c:c + 1]

        def load_slab(w2d, c0, ncols):
            s = ctr["slot"] % 3
            ctr["slot"] += 1
            src = w2d.rearrange("(k p) c -> p k c", p=128)[:, :, c0:c0 + ncols]
            dst = slotA[:, s, :, 0:ncols]
            S.add("pool", lambda e: e.dma_start(out=dst, in_=src), writes=[slotB[s]], dma=True)
            return s

        def load_regB(w2d, r0, nchunks):
            src = w2d[r0 * 128:(r0 + nchunks) * 128, :].rearrange("(j p) c -> p j c", p=128)
            dst = regB[:, 0:nchunks, :]
            S.add("pool", lambda e: e.dma_start(out=dst, in_=src), writes=[regBB], dma=True)

        xin = [carve(PH0 + i * 4096, 4096, F32) for i in range(2)]
        xinB = [Buf(f"xin{i}") for i in range(2)]

        def load_x(s):
            for tt in range(SEQ // 128):
                sl = tt % 2
                r0 = s * SEQ + tt * 128
                S.add("sp", lambda e, sl=sl, r0=r0: e.dma_start(out=xin[sl], in_=x_d[r0:r0 + 128, :]),
                      writes=[xinB[sl]], dma=True)
                t = tt // 4
                for half in range(2):
                    b = (tt * 2 + half) % 8
                    for mm_ in range(4):
                        m = half * 4 + mm_
                        S.add("pe", lambda e, b=b, mm_=mm_, m=m, sl=sl: e.transpose(
                            psum[:, b, mm_ * 128:(mm_ + 1) * 128], xin[sl][:, m * 128:(m + 1) * 128], ident[:]),
                            reads=[xinB[sl], cstB], writes=[bankB[b]])
                    dst = x_sb[:, half * 4:half * 4 + 4, tt * 128:(tt + 1) * 128]
                    srcp = psum[:, b, :].rearrange("p (a c) -> p a c", a=4)
                    eng = "act" if half == 0 else "dve"
                    if eng == "act":
                        S.add("act", lambda e, dst=dst, srcp=srcp: e.activation(out=dst, in_=srcp, func=AF.Copy),
                              reads=[bankB[b]], writes=[xB[half * 4 + i][t] for i in range(4)])
                    else:
                        S.add("dve", lambda e, dst=dst, srcp=srcp: e.tensor_copy(out=dst, in_=srcp),
                              reads=[bankB[b]], writes=[xB[half * 4 + i][t] for i in range(4)])

        fin_g = carve(PH0 + 8192, 4096, F32)
        fin_gB = Buf("fin_g")
        ot_sb = [carve(PH0 + 12288 + i * 4096, 4096, F32) for i in range(2)]
        otB = [Buf(f"ot{i}") for i in range(2)]
        fsq = carve(PH0 + 20480, 4096, F32)
        fsqB = Buf("fsq")
        fss = [carve(PH0 + 24576 + i * 16, 4, F32) for i in range(4)]
        fssB = [Buf(f"fss{i}") for i in range(4)]
        out_ops = []

        def final_store(s):
            S.add("sp", lambda e: e.dma_start(out=fin_g, in_=fin_g_d), writes=[fin_gB], dma=True)
            for tt in range(SEQ // 128):
                t = tt // 4
                pb = (tt % 4) * 2
                for m in range(8):
                    b = pb + m // 4
                    S.add("pe", lambda e, b=b, m=m, tt=tt: e.transpose(
                        psum[:, b, (m % 4) * 128:(m % 4 + 1) * 128], x_sb[:, m, tt * 128:(tt + 1) * 128], ident[:]),
                        reads=[xB[m][t], cstB], writes=[bankB[b]])
                pv = psum[:, pb:pb + 2, :].rearrange("p a c -> p (a c)")
                i2 = tt % 2
                ss, ssB = fss[i2], fssB[i2]
                sd1, sd1B = fss[2 + i2], fssB[2 + i2]
                S.add("act", lambda e, pv=pv: e.activation(out=fsq, in_=pv, func=AF.Square),
                      reads=[bankB[pb], bankB[pb + 1]], writes=[fsqB])
                S.add("dve", lambda e, ss=ss: e.reduce_sum(out=ss, in_=fsq, axis=AX.X), reads=[fsqB], writes=[ssB])
                S.add("act", lambda e, ss=ss, sd1=sd1: e.activation(out=sd1, in_=ss, func=AF.Sqrt, bias=EPS, scale=1.0 / D),
                      reads=[ssB], writes=[sd1B])
                S.add("dve", lambda e, sd1=sd1: e.reciprocal(out=sd1, in_=sd1), reads=[sd1B], writes=[sd1B])
                o = ot_sb[i2]
                S.add("dve", lambda e, o=o, pv=pv, sd1=sd1: e.scalar_tensor_tensor(
                    out=o, in0=pv, scalar=sd1, in1=fin_g, op0=ALU.mult, op1=ALU.mult),
                    reads=[bankB[pb], bankB[pb + 1], sd1B, fin_gB], writes=[otB[i2]])
                r0 = s * SEQ + tt * 128
                out_ops.append(S.add("sp", lambda e, o=o, r0=r0: e.dma_start(out=out_d[r0:r0 + 128, :], in_=o),
                                     reads=[otB[i2]], dma=True))

        def zero_hpad():
            S.add("pool", lambda e: e.memset(h_sb[:, :, 0:PAD], 0.0), writes=[hpadB])
            S.add("pool", lambda e: e.memset(h_sb[:, :, PAD + SEQ:HW], 0.0), writes=[hpadB])

        def norm_phase(gcol0):
            zero_hpad()
            for t in range(NT):
                c0 = t * TW
                nb = t % 2
                for m in range(8):
                    i = ctr["sq"] % 4
                    ctr["sq"] += 1
                    S.add("act", lambda e, i=i, m=m, c0=c0: e.activation(out=sq_sb[i], in_=x_sb[:, m, c0:c0 + TW], func=AF.Square),
                          reads=[xB[m][t]], writes=[sqB[i]])
                    S.add("pe", lambda e, i=i, m=m, nb=nb: e.matmul(bank(nb), lhsT=ones_bf, rhs=sq_sb[i], start=(m == 0), stop=(m == 7)),
                          reads=[sqB[i], cst2B], writes=[bankB[nb]])
                i = ctr["sd"] % 2
                ctr["sd"] += 1
                S.add("act", lambda e, i=i, nb=nb: e.activation(out=sd_sb[i], in_=bank(nb), func=AF.Sqrt, bias=EPS, scale=1.0 / D),
                      reads=[bankB[nb]], writes=[sdB[i]])
                S.add("dve", lambda e, i=i: e.reciprocal(out=rstd_sb[i], in_=sd_sb[i]), reads=[sdB[i]], writes=[rstdB[i]])
                for m in range(8):
                    S.add("dve", lambda e, i=i, m=m, c0=c0: e.scalar_tensor_tensor(
                        out=h_sb[:, m, PAD + c0:PAD + c0 + TW], in0=x_sb[:, m, c0:c0 + TW], scalar=pcol(gcol0 + m),
                        in1=rstd_sb[i], op0=ALU.mult, op1=ALU.mult),
                        reads=[xB[m][t], rstdB[i], parB], writes=[hB[m][t]])

        def hcols(k, t):
            return h_sb[:, k, PAD + t * TW:PAD + (t + 1) * TW]

        act_sb = carve(PH0, 8 * SEQ * 2, BF16).rearrange("p (j t) -> p j t", j=8)
        sg_sb = [carve(PH0 + 8 * SEQ * 2 + i * 2048, 2048, F32) for i in range(3)]
        actB = [[Buf(f"act{j}_{t}") for t in range(NT)] for j in range(8)]
        sgB = [Buf(f"sg{i}") for i in range(3)]
        ctr["sg"] = 0
        ctr["gu"] = 0

        def ffn_phase(l, which):
            w_in = w_in_d[which][l]
            w_out = w_out_d[which][l]
            w_in_v = w_in.rearrange("(k p) c -> p k c", p=128)
            pairs = [(ti, j0 + 2 * pi) for ti, (j0, nj) in enumerate(THIRDS) for pi in range(nj // 2)]
            slots = {}

            def issue(gi):
                if gi >= len(pairs):
                    return
                s = ctr["slot"] % 3
                ctr["slot"] += 1
                jj = pairs[gi][1]
                srcg = w_in_v[:, :, jj * 128:jj * 128 + 256]
                srcu = w_in_v[:, :, DFF + jj * 128:DFF + jj * 128 + 256]
                S.add("pool", lambda e: e.dma_start(out=slotA[:, s, :, 0:256], in_=srcg), writes=[slotB[s]], dma=True)
                S.add("pool", lambda e: e.dma_start(out=slotA[:, s, :, 256:512], in_=srcu), writes=[slotUB[s]], dma=True)
                slots[gi] = s

            issue(0)
            issue(1)
            norm_phase(PC_NORM(l, 0 if which == 1 else 2))
            gi = 0
            for ti, (j0, nj) in enumerate(THIRDS):
                npairs = nj // 2
                load_regB(w_out, j0, nj)
                for pi in range(npairs):
                    s = slots[gi]
                    issue(gi + 2)
                    gi += 1
                    for t in range(NT):
                        for jj in range(2):
                            jl = 2 * pi + jj
                            gb = (ctr["gu"] % 3) * 2 + 2
                            ub = gb + 1
                            ctr["gu"] += 1
                            for k in range(8):
                                S.add("pe", lambda e, s=s, k=k, jj=jj, gb=gb, t=t: e.matmul(
                                    bank(gb), lhsT=slotA[:, s, k, jj * 128:(jj + 1) * 128], rhs=hcols(k, t), start=(k == 0), stop=(k == 7)),
                                    reads=[slotB[s], hB[k][t]], writes=[bankB[gb]])
                            for k in range(8):
                                S.add("pe", lambda e, s=s, k=k, jj=jj, ub=ub, t=t: e.matmul(
                                    bank(ub), lhsT=slotA[:, s, k, 256 + jj * 128:256 + (jj + 1) * 128], rhs=hcols(k, t), start=(k == 0), stop=(k == 7)),
                                    reads=[slotUB[s], hB[k][t]], writes=[bankB[ub]])
                            i = ctr["sg"] % 3
                            ctr["sg"] += 1
                            S.add("act", lambda e, i=i, gb=gb: e.activation(out=sg_sb[i], in_=bank(gb), func=AF.Silu),
                                  reads=[bankB[gb]], writes=[sgB[i]])
                            S.add("dve", lambda e, i=i, ub=ub, jl=jl, t=t: e.tensor_tensor(
                                out=act_sb[:, jl, t * TW:(t + 1) * TW], in0=sg_sb[i], in1=bank(ub), op=ALU.mult),
                                reads=[sgB[i], bankB[ub]], writes=[actB[jl][t]])
                for t in range(NT):
                    for m in range(8):
                        ob = m % 2
                        for jl in range(nj):
                            S.add("pe", lambda e, ob=ob, jl=jl, m=m, t=t, nj=nj: e.matmul(
                                bank(ob), lhsT=regB[:, jl, m * 128:(m + 1) * 128], rhs=act_sb[:, jl, t * TW:(t + 1) * TW],
                                start=(jl == 0), stop=(jl == nj - 1)),
                                reads=[regBB, actB[jl][t]], writes=[bankB[ob]])
                        S.add("dve", lambda e, ob=ob, m=m, t=t: e.scalar_tensor_tensor(
                            out=x_sb[:, m, t * TW:(t + 1) * TW], in0=bank(ob), scalar=0.5, in1=x_sb[:, m, t * TW:(t + 1) * TW],
                            op0=ALU.mult, op1=ALU.add),
                            reads=[bankB[ob], xB[m][t]], writes=[xB[m][t]])

        TB = [Buf(f"T{i}") for i in range(4)]
        ctr["T"] = 0
        ctr["ss"] = 0

        def issue_slab(w2d, c0, ncols):
            s = ctr["slot"] % 3
            ctr["slot"] += 1
            src = w2d.rearrange("(k p) c -> p k c", p=128)[:, :, c0:c0 + ncols]
            S.add("pool", lambda e: e.dma_start(out=slotA[:, s, :, 0:ncols], in_=src), writes=[slotB[s], slotUB[s]], dma=True)
            return s

        def proj(bk, s, col0, t, ncol=128):
            for k in range(8):
                S.add("pe", lambda e, k=k: e.matmul(bank(bk), lhsT=slotA[:, s, k, col0:col0 + ncol], rhs=hcols(k, t),
                                                   start=(k == 0), stop=(k == 7)),
                      reads=[slotB[s], hB[k][t]], writes=[bankB[bk]])

        def out_proj(cat, catB, t, banks):
            for m in range(8):
                ob = banks[m % len(banks)]
                for c in range(8):
                    S.add("pe", lambda e, ob=ob, c=c, m=m: e.matmul(
                        bank(ob), lhsT=regB[:, c, m * 128:(m + 1) * 128], rhs=cat[:, c, :], start=(c == 0), stop=(c == 7)),
                        reads=[regBB, catB[c]], writes=[bankB[ob]])
                S.add("dve", lambda e, ob=ob, m=m: e.tensor_tensor(
                    out=x_sb[:, m, t * TW:(t + 1) * TW], in0=bank(ob), in1=x_sb[:, m, t * TW:(t + 1) * TW], op=ALU.add),
                    reads=[bankB[ob], xB[m][t]], writes=[xB[m][t]])

        o = PH0
        kT_sb = carve(o, 2 * SEQ * 2, BF16).rearrange("p (c t) -> p c t", c=2); o += 2 * SEQ * 2
        vaug = carve(o, 16 * 2 * 192 * 2, BF16).rearrange("p (a b c) -> p a b c", a=16, b=2); o += 16 * 2 * 192 * 2
        rope_sb = [carve(o + i * 4096, 4096, F32).rearrange("p (a t) -> p a t", a=2) for i in range(2)]; o += 8192
        qT_sb = carve(o, 4 * TW * 2, BF16).rearrange("p (c t) -> p c t", c=4); o += 4 * TW * 2
        cat_e = carve(o, 8 * TW * 2, BF16).rearrange("p (c t) -> p c t", c=8); o += 8 * TW * 2
        ybuf = [carve(o + i * 2064, 2064, F32) for i in range(2)]; o += 4128
        halo_sb = carve(o, 64, F32); o += 64
        PT_sb = [carve(o + i * 1024, 1024, BF16) for i in range(4)]; o += 4096
        T_e = [carve(o + i * 2048, 2048, F32) for i in range(4)]; o += 8192
        assert o <= ARENA_BYTES, o
        kTB = [[Buf(f"kT{c}_{t}") for t in range(NT)] for c in range(2)]
        vaugB = [Buf(f"vaug{t}") for t in range(NT)]
        vonesB = Buf("vones")
        ropeB = [Buf(f"rope{i}") for i in range(2)]
        qTB = [Buf(f"qT{c}") for c in range(4)]
        catB = [Buf(f"cat{c}") for c in range(8)]
        ybufB = [Buf(f"ybuf{i}") for i in range(2)]
        haloB = Buf("halo")
        PTB = [Buf(f"PT{i}") for i in range(4)]
        ctr["rope"] = 0
        ctr["pt"] = 0
        ctr["yb"] = 0
        ctr["pair"] = 0

        def next_pair():
            gb = (ctr["pair"] % 3) * 2 + 2
            ctr["pair"] += 1
            return gb, gb + 1

        def rope_apply(T, pb, sbk, g, gs, rp, out_ap, outB):
            i = ctr["sq"] % 4
            ctr["sq"] += 1
            S.add("act", lambda e: e.activation(out=sq_sb[i], in_=bank(pb), func=AF.Square), reads=[bankB[pb]], writes=[sqB[i]])
            S.add("pe", lambda e: e.matmul(bank(0), lhsT=blk_bf, rhs=sq_sb[i], start=True, stop=True),
                  reads=[sqB[i], cst2B], writes=[bankB[0]])
            d = ctr["sd"] % 2
            ctr["sd"] += 1
            S.add("act", lambda e: e.activation(out=sd_sb[d], in_=bank(0), func=AF.Sqrt, bias=EPS, scale=1.0 / 64),
                  reads=[bankB[0]], writes=[sdB[d]])
            S.add("dve", lambda e: e.reciprocal(out=rstd_sb[d], in_=sd_sb[d]), reads=[sdB[d]], writes=[rstdB[d]])
            S.add("dve", lambda e: e.scalar_tensor_tensor(out=T[0], in0=bank(pb), scalar=g, in1=rope_sb[rp][:, 0, :],
                                                           op0=ALU.mult, op1=ALU.mult),
                  reads=[bankB[pb], parB, ropeB[rp]], writes=[TB[0]])
            S.add("dve", lambda e: e.scalar_tensor_tensor(out=T[1], in0=bank(sbk), scalar=gs, in1=rope_sb[rp][:, 1, :],
                                                           op0=ALU.mult, op1=ALU.mult),
                  reads=[bankB[sbk], parB, ropeB[rp]], writes=[TB[1]])
            S.add("pool", lambda e: e.tensor_tensor(out=T[0], in0=T[0], in1=T[1], op=ALU.add), reads=[TB[0], TB[1]], writes=[TB[0]])
            S.add("pool", lambda e: e.tensor_tensor(out=out_ap, in0=T[0], in1=rstd_sb[d], op=ALU.mult),
                  reads=[TB[0], rstdB[d]], writes=[outB])

        def load_rope(t):
            rp = ctr["rope"] % 2
            ctr["rope"] += 1
            S.add("sp", lambda e: e.dma_start(out=rope_sb[rp], in_=rope_d[:, :, t * TW:(t + 1) * TW]), writes=[ropeB[rp]], dma=True)
            return rp

        def even_mixer(l):
            j = l // 2
            w_in = ev_w_in_d[j]
            gq, gqs, gk, gks = [pcol(PC_QK + j * 4 + i) for i in range(4)]
            T = T_e
            sK = issue_slab(w_in, 20 * 128, 512)
            sV = issue_slab(w_in, 24 * 128, 128)
            load_regB(ev_w_out_d[j], 0, 8)
            S.add("pool", lambda e: e.memset(vaug[:, :, :, 0:64], 1.0), writes=[vonesB])
            S.add("pool", lambda e: e.memset(vaug[:, :, :, 128:192], 1.0), writes=[vonesB])
            norm_phase(PC_NORM(l, 1))
            for t in range(NT):
                rp = load_rope(t)
                for kc in range(2):
                    gb, ub = next_pair()
                    proj(gb, sK, kc * 128, t)
                    proj(ub, sK, (2 + kc) * 128, t)
                    rope_apply(T, gb, ub, gk, gks, rp, kT_sb[:, kc, t * TW:(t + 1) * TW], kTB[kc][t])
                for sub in range(4):
                    for k in range(8):
                        S.add("pe", lambda e, k=k, sub=sub, t=t: e.matmul(
                            psum[:, 1, sub * 128:(sub + 1) * 128],
                            lhsT=h_sb[:, k, PAD + t * TW + sub * 128:PAD + t * TW + (sub + 1) * 128],
                            rhs=slotA[:, sV, k, 0:128], start=(k == 0), stop=(k == 7)),
                            reads=[slotB[sV], hB[k][t]], writes=[bankB[1]])
                for kv in range(2):
                    srcv = psum[:, 1, :].rearrange("p (s c) -> p s c", s=4)[:, :, kv * 64:(kv + 1) * 64]
                    dstv = vaug[:, t * 4:(t + 1) * 4, kv, 64:128]
                    S.add("act", lambda e, srcv=srcv, dstv=dstv: e.activation(out=dstv, in_=srcv, func=AF.Copy),
                          reads=[bankB[1]], writes=[vaugB[t]])
            sQ = issue_slab(w_in, 12 * 128, 512)
            sQS = issue_slab(w_in, 16 * 128, 512)
            sC = issue_slab(w_in, 4 * 128, 512)
            for t in range(NT):
                rp = load_rope(t)
                for qc in range(4):
                    gb, ub = next_pair()
                    proj(gb, sQ, qc * 128, t)
                    proj(ub, sQS, qc * 128, t)
                    rope_apply(T, gb, ub, gq, gqs, rp, qT_sb[:, qc, :], qTB[qc])
                sH = issue_slab(w_in, 8 * 128, 512)
                sBg = issue_slab(w_in, 0, 512)
                a0 = PAD + t * TW - 1
                for c in range(4):
                    gb, ub = next_pair()
                    proj(gb, sC, c * 128, t)
                    proj(ub, sH, c * 128, t)
                    for (sl_, hc0) in ((sC, c * 4), (sH, c * 4 + 2)):
                        for k in range(8):
                            S.add("pe", lambda e, k=k, sl_=sl_, hc0=hc0, c=c, a0=a0: e.matmul(
                                psum[:, 1, hc0:hc0 + 2], lhsT=slotA[:, sl_, k, c * 128:(c + 1) * 128],
                                rhs=h_sb[:, k, a0:a0 + 514:513], start=(k == 0), stop=(k == 7)),
                                reads=[slotB[sl_], hB[k][t], hpadB] + ([hB[k][t - 1]] if t > 0 else []) + ([hB[k][t + 1]] if t < NT - 1 else []),
                                writes=[bankB[1]])
                    yi = ctr["yb"] % 2
                    ctr["yb"] += 1
                    yb = ybuf[yi]
                    S.add("act", lambda e, gb=gb: e.activation(out=T[2], in_=bank(gb), func=AF.Copy), reads=[bankB[gb]], writes=[TB[2]])
                    S.add("dve", lambda e, ub=ub, yb=yb: e.tensor_tensor(out=yb[:, 1:513], in0=T[2], in1=bank(ub), op=ALU.mult),
                          reads=[TB[2], bankB[ub]], writes=[ybufB[yi]])
                    S.add("act", lambda e, c=c: e.activation(out=halo_sb[:, c * 4:c * 4 + 2], in_=psum[:, 1, c * 4:c * 4 + 2], func=AF.Copy),
                          reads=[bankB[1]], writes=[haloB])
                    S.add("dve", lambda e, c=c, yb=yb: e.tensor_tensor(out=yb[:, 0:514:513], in0=halo_sb[:, c * 4:c * 4 + 2],
                                                                         in1=psum[:, 1, c * 4 + 2:c * 4 + 4], op=ALU.mult),
                          reads=[haloB, bankB[1], ybufB[yi]], writes=[ybufB[yi]])
                    w0, w1, w2 = [pcol(PC_CONV + j * 12 + s_ * 4 + c) for s_ in range(3)]
                    S.add("pool", lambda e, yb=yb, w0=w0: e.tensor_scalar(out=T[3], in0=yb[:, 0:512], scalar1=w0, scalar2=None, op0=ALU.mult),
                          reads=[ybufB[yi], parB], writes=[TB[3]])
                    S.add("dve", lambda e, yb=yb, w1=w1: e.scalar_tensor_tensor(out=T[3], in0=yb[:, 1:513], scalar=w1, in1=T[3],
                                                                                  op0=ALU.mult, op1=ALU.add),
                          reads=[ybufB[yi], parB, TB[3]], writes=[TB[3]])
                    S.add("dve", lambda e, yb=yb, w2=w2: e.scalar_tensor_tensor(out=T[3], in0=yb[:, 2:514], scalar=w2, in1=T[3],
                                                                                  op0=ALU.mult, op1=ALU.add),
                          reads=[ybufB[yi], parB, TB[3]], writes=[TB[3]])
                    gb2, _ = next_pair()
                    proj(gb2, sBg, c * 128, t)
                    S.add("dve", lambda e, gb2=gb2, c=c: e.tensor_tensor(out=cat_e[:, c, :], in0=T[3], in1=bank(gb2), op=ALU.mult),
                          reads=[TB[3], bankB[gb2]], writes=[catB[c]])
                if t < NT - 1:
                    sQ = issue_slab(w_in, 12 * 128, 512)
                    sQS = issue_slab(w_in, 16 * 128, 512)
                    sC = issue_slab(w_in, 4 * 128, 512)
                for qc in range(4):
                    kv = qc // 2
                    for par_ in range(2):
                        ob = 6 + par_
                        r0 = par_ * 64
                        voff = 64 if par_ == 0 else 0
                        nr0 = r0
                        dr0 = 64 - r0
                        sbanks = {}
                        pts = {}

                        def s_mm(kt):
                            sbk = 2 + (ctr["pt"] % 4)
                            pi = ctr["pt"] % 4
                            ctr["pt"] += 1
                            sbanks[kt] = sbk
                            pts[kt] = pi
                            S.add("pe", lambda e, r0=r0, kv=kv, qc=qc: e.matmul(bank(sbk), lhsT=kT_sb[r0:r0 + 64, kv, kt * 128:(kt + 1) * 128],
                                                           rhs=qT_sb[r0:r0 + 64, qc, :], start=True, stop=True),
                                  reads=[kTB[kv][kt // 4], qTB[qc]], writes=[bankB[sbk]])
                            S.add("act", lambda e: e.activation(out=PT_sb[pi], in_=bank(sbk), func=AF.Exp, scale=0.125),
                                  reads=[bankB[sbk]], writes=[PTB[pi]])

                        def pv_mm(kt):
                            pi = pts[kt]
                            S.add("pe", lambda e, ob=ob, kv=kv, voff=voff: e.matmul(bank(ob), lhsT=vaug[:, kt, kv, voff:voff + 128], rhs=PT_sb[pi],
                                                           start=(kt == 0), stop=(kt == 15)),
                                  reads=[vaugB[kt // 4], vonesB, PTB[pi]], writes=[bankB[ob]])

                        s_mm(0)
                        s_mm(1)
                        for kt in range(16):
                            if kt + 2 < 16:
                                s_mm(kt + 2)
                            pv_mm(kt)
                        ti = par_
                        S.add("dve", lambda e, ti=ti, dr0=dr0, ob=ob: e.reciprocal(out=T[ti][dr0:dr0 + 64, :], in_=bank(ob)[dr0:dr0 + 64, :]),
                              reads=[bankB[ob]], writes=[TB[ti]])
                        S.add("dve", lambda e, ti=ti, dr0=dr0, nr0=nr0, ob=ob, qc=qc: e.tensor_tensor(out=cat_e[nr0:nr0 + 64, 4 + qc, :], in0=bank(ob)[nr0:nr0 + 64, :],
                                                                        in1=T[ti][dr0:dr0 + 64, :], op=ALU.mult),
                              reads=[bankB[ob], TB[ti], catB[4 + qc]], writes=[catB[4 + qc]])
                out_proj(cat_e, catB, t, (2, 3, 4, 5))

        o = PH0
        pbuf = carve(o, 4 * 528 * 4, F32).rearrange("p (g t) -> p g t", g=4); o += 4 * 528 * 4
        gu_sb = carve(o, 4 * 2048, F32).rearrange("p (g t) -> p g t", g=4); o += 4 * 2048
        T_o = [carve(o + i * 2048, 2048, F32) for i in range(4)]; o += 8192
        vn_sb = carve(o, 4096, BF16).rearrange("p (s c) -> p s c", s=4); o += 4096
        pw_sb = [carve(o + i * 2112, 2112, F32) for i in range(2)]; o += 4224
        pooled_sb = [carve(o + i * 1024, 1024, BF16) for i in range(2)]; o += 2048
        cat_o = carve(o, 8 * TW * 2, BF16).rearrange("p (c t) -> p c t", c=8); o += 8 * TW * 2
        sgug_sb = carve(o, 2048, F32); o += 2048
        sgub_sb = carve(o, 8192, F32).rearrange("p (g t) -> p g t", g=4); o += 8192
        poolw_sb = carve(o, 1024, BF16).rearrange("p (g d) -> p g d", g=4); o += 1024
        wst_sb = carve(o, 1024, BF16).rearrange("p (g d) -> p g d", g=4); o += 1024
        edge_sb = carve(o, 256, F32).rearrange("p (g i) -> p g i", g=4); o += 256
        vss_sb = [carve(o + i * 16, 4, F32) for i in range(4)]; o += 64
        assert o <= ARENA_BYTES, o
        pbufB = [Buf(f"pbuf{g}") for g in range(4)]
        guB = [Buf(f"gu{g}") for g in range(4)]
        vnB = [Buf(f"vn{s}") for s in range(4)]
        pwB = [Buf(f"pw{i}") for i in range(2)]
        pooledB = [Buf(f"pooled{i}") for i in range(2)]
        otabB = Buf("otab")
        otab2B = Buf("otab2")
        vssB = [Buf(f"vss{i}") for i in range(4)]
        ctr["pl"] = 0
        ctr["vss"] = 0
        GC1 = 0.044715
        GC2 = 1.5957691216057308

        def gelu(T, src_ps, srcB, out_ap, outB, ta, tb):
            S.add("act", lambda e: e.activation(out=T[ta], in_=src_ps, func=AF.Square), reads=[srcB], writes=[TB[ta]])
            S.add("dve", lambda e: e.tensor_scalar(out=T[tb], in0=T[ta], scalar1=GC1, scalar2=1.0, op0=ALU.mult, op1=ALU.add),
                  reads=[TB[ta]], writes=[TB[tb]])
            S.add("dve", lambda e: e.tensor_tensor(out=T[tb], in0=T[tb], in1=src_ps, op=ALU.mult), reads=[TB[tb], srcB], writes=[TB[tb]])
            S.add("act", lambda e: e.activation(out=T[ta], in_=T[tb], func=AF.Sigmoid, scale=GC2), reads=[TB[tb]], writes=[TB[ta]])
            S.add("dve", lambda e: e.tensor_tensor(out=out_ap, in0=T[ta], in1=src_ps, op=ALU.mult), reads=[TB[ta], srcB], writes=[outB])

        def odd_mixer(l):
            j = l // 2
            w_in = od_w_in_d[j]
            T = T_o
            sP = issue_slab(w_in, 0, 512)
            sU = issue_slab(w_in, 512, 512)
            sVv = issue_slab(w_in, 1024, 512)
            load_regB(od_w_out_d[j], 0, 8)
            S.add("sp", lambda e: e.dma_start(out=sgug_sb, in_=sgu_g_d[j]), writes=[otabB], dma=True)
            S.add("sp", lambda e: e.dma_start(out=sgub_sb, in_=sgu_b_d[j]), writes=[otabB], dma=True)
            S.add("sp", lambda e: e.dma_start(out=edge_sb, in_=edge_d), writes=[otabB], dma=True)
            S.add("pool", lambda e: e.dma_start(out=poolw_sb, in_=pool_w_d[j]), writes=[otab2B], dma=True)
            S.add("pool", lambda e: e.dma_start(out=wst_sb, in_=sgu_wt_d[j]), writes=[otab2B], dma=True)
            norm_phase(PC_NORM(l, 1))
            for t in range(NT):
                a = t * TW
                for g in range(4):
                    r = (1, 2, 4, 8)[g]
                    gb, _ = next_pair()
                    proj(gb, sP, g * 128, t)
                    for (hc0, c0) in ((g * 16, a), (g * 16 + 8, a + 520)):
                        for k in range(8):
                            S.add("pe", lambda e, k=k, hc0=hc0, c0=c0, g=g: e.matmul(
                                psum[:, 1, hc0:hc0 + 8], lhsT=slotA[:, sP, k, g * 128:(g + 1) * 128],
                                rhs=h_sb[:, k, c0:c0 + 8], start=(k == 0), stop=(k == 7)),
                                reads=[slotB[sP], hB[k][t], hpadB] + ([hB[k][t - 1]] if t > 0 else []) + ([hB[k][t + 1]] if t < NT - 1 else []),
                                writes=[bankB[1]])
                    P = pbuf[:, g, :]
                    S.add("act", lambda e, gb=gb, P=P: e.activation(out=P[:, 8:520], in_=bank(gb), func=AF.Copy), reads=[bankB[gb]], writes=[pbufB[g]])
                    S.add("act", lambda e, g=g, P=P: e.activation(out=P[:, 0:8], in_=psum[:, 1, g * 16:g * 16 + 8], func=AF.Copy),
                          reads=[bankB[1], pbufB[g]], writes=[pbufB[g]])
                    S.add("act", lambda e, g=g, P=P: e.activation(out=P[:, 520:528], in_=psum[:, 1, g * 16 + 8:g * 16 + 16], func=AF.Copy),
                          reads=[bankB[1], pbufB[g]], writes=[pbufB[g]])
                    A_, B_ = pw_sb[0], pw_sb[1]

                    def padd(out_ap, in0, in1, rB, wB):
                        S.add("pool", lambda e: e.tensor_tensor(out=out_ap, in0=in0, in1=in1, op=ALU.add), reads=rB, writes=wB)

                    padd(A_[:, 0:527], P[:, 0:527], P[:, 1:528], [pbufB[g]], [pwB[0]])
                    if r == 1:
                        padd(T[3], A_[:, 7:519], P[:, 9:521], [pwB[0], pbufB[g]], [TB[3]])
                    else:
                        padd(B_[:, 0:525], A_[:, 0:525], A_[:, 2:527], [pwB[0]], [pwB[1]])
                        if r == 2:
                            padd(T[3], B_[:, 6:518], P[:, 10:522], [pwB[1], pbufB[g]], [TB[3]])
                        else:
                            padd(A_[:, 0:521], B_[:, 0:521], B_[:, 4:525], [pwB[1], pwB[0]], [pwB[0]])
                            if r == 4:
                                padd(T[3], A_[:, 4:516], P[:, 12:524], [pwB[0], pbufB[g]], [TB[3]])
                            else:
                                padd(B_[:, 0:513], A_[:, 0:513], A_[:, 8:521], [pwB[0], pwB[1]], [pwB[1]])
                                padd(T[3], B_[:, 0:512], P[:, 16:528], [pwB[1], pbufB[g]], [TB[3]])
                    pi = ctr["pl"] % 2
                    ctr["pl"] += 1
                    pl = pooled_sb[pi]
                    inv = 1.0 / (2 * r + 1)
                    S.add("dve", lambda e, pl=pl, P=P, inv=inv: e.scalar_tensor_tensor(out=pl, in0=T[3], scalar=inv, in1=P[:, 8:520],
                                                                                       op0=ALU.mult, op1=ALU.subtract),
                          reads=[TB[3], pbufB[g]], writes=[pooledB[pi]])
                    for (cond, c0, e0) in ((t == 0, 0, 0), (t == NT - 1, 504, 8)):
                        if cond:
                            S.add("dve", lambda e, c0=c0, e0=e0, g=g: e.tensor_tensor(out=T[3][:, c0:c0 + 8], in0=T[3][:, c0:c0 + 8],
                                                                                        in1=edge_sb[:, g, e0:e0 + 8], op=ALU.mult),
                                  reads=[TB[3], otabB, pooledB[pi]], writes=[TB[3]])
                            S.add("dve", lambda e, c0=c0, pl=pl, P=P: e.tensor_tensor(out=pl[:, c0:c0 + 8], in0=T[3][:, c0:c0 + 8],
                                                                                        in1=P[:, 8 + c0:16 + c0], op=ALU.subtract),
                                  reads=[TB[3], pbufB[g], pooledB[pi]], writes=[pooledB[pi]])
                    cb = 6 + (g % 2)
                    S.add("pe", lambda e, cb=cb, pl=pl, g=g: e.matmul(bank(cb), lhsT=poolw_sb[:, g, :], rhs=pl, start=True, stop=True),
                          reads=[otab2B, pooledB[pi]], writes=[bankB[cb]])
                    S.add("act", lambda e, cb=cb, g=g: e.activation(out=cat_o[:, g, :], in_=bank(cb), func=AF.Copy, scale=pcol(PC_PSC + j * 4 + g)),
                          reads=[bankB[cb], parB], writes=[catB[g]])
                for g in range(4):
                    gb, _ = next_pair()
                    proj(gb, sU, g * 128, t)
                    gelu(T, bank(gb), bankB[gb], gu_sb[:, g, :], guB[g], 0, 1)
                for sub in range(4):
                    vb, _ = next_pair()
                    for k in range(8):
                        S.add("pe", lambda e, k=k, vb=vb, sub=sub, t=t: e.matmul(
                            bank(vb), lhsT=h_sb[:, k, PAD + t * TW + sub * 128:PAD + t * TW + (sub + 1) * 128],
                            rhs=slotA[:, sVv, k, 0:512], start=(k == 0), stop=(k == 7)),
                            reads=[slotB[sVv], hB[k][t]], writes=[bankB[vb]])
                    gelu(T, bank(vb), bankB[vb], T[2], TB[2], 0, 1)
                    vi = ctr["vss"] % 2
                    ctr["vss"] += 1
                    ss, ssB_ = vss_sb[vi], vssB[vi]
                    sd1, sd1B = vss_sb[2 + vi], vssB[2 + vi]
                    S.add("act", lambda e: e.activation(out=T[0], in_=T[2], func=AF.Square), reads=[TB[2]], writes=[TB[0]])
                    S.add("dve", lambda e, ss=ss: e.reduce_sum(out=ss, in_=T[0], axis=AX.X), reads=[TB[0]], writes=[ssB_])
                    S.add("act", lambda e, ss=ss, sd1=sd1: e.activation(out=sd1, in_=ss, func=AF.Sqrt, bias=EPS, scale=1.0 / 512),
                          reads=[ssB_], writes=[sd1B])
                    S.add("dve", lambda e, sd1=sd1: e.reciprocal(out=sd1, in_=sd1), reads=[sd1B], writes=[sd1B])
                    S.add("dve", lambda e, sd1=sd1, sub=sub: e.scalar_tensor_tensor(out=vn_sb[:, sub, :], in0=T[2], scalar=sd1, in1=sgug_sb,
                                                                                      op0=ALU.mult, op1=ALU.mult),
                          reads=[TB[2], sd1B, otabB], writes=[vnB[sub]])
                for g in range(4):
                    mb = 6 + (g % 2)
                    for sub in range(4):
                        S.add("pe", lambda e, mb=mb, sub=sub, g=g: e.matmul(
                            psum[:, mb, sub * 128:(sub + 1) * 128], lhsT=vn_sb[:, sub, g * 128:(g + 1) * 128], rhs=wst_sb[:, g, :],
                            start=True, stop=True),
                            reads=[vnB[sub], otab2B], writes=[bankB[mb]])
                    S.add("dve", lambda e, mb=mb, g=g: e.tensor_tensor(out=T[3], in0=bank(mb), in1=sgub_sb[:, g, :], op=ALU.add),
                          reads=[bankB[mb], otabB], writes=[TB[3]])
                    S.add("pool", lambda e, g=g: e.tensor_tensor(out=cat_o[:, 4 + g, :], in0=T[3], in1=gu_sb[:, g, :], op=ALU.mult),
                          reads=[TB[3], guB[g]], writes=[catB[4 + g]])
                out_proj(cat_o, catB, t, (2, 3, 4, 5))


        slotUB = [Buf(f"slotU{i}") for i in range(3)]

        for s in range(nseq):
            S.barrier()
            load_x(s)
            S.barrier()
            prev = None
            for l in layers:
                for ph in phases:
                    kind = "ffn" if ph.startswith("ffn") else "mix"
                    if prev is not None and (kind != prev or kind == "mix"):
                        S.barrier()
                    prev = kind
                    if ph == "ffn1":
                        ffn_phase(l, 1)
                    elif ph == "ffn2":
                        ffn_phase(l, 2)
                    elif l % 2 == 0:
                        even_mixer(l)
                    else:
                        odd_mixer(l)
            S.barrier()
            final_store(s)
        fin = S.add("sp", None, reads=[], writes=[])
        fin.deps = list(out_ops)

        block = stack.enter_context(nc.Block())
        S.emit(nc, stack, block)
    return nc, S


def _rope_tables():
    rows = SEQ // 64
    r_idx, c_idx = np.meshgrid(np.arange(rows), np.arange(64), indexing="ij")
    r_idx = r_idx.reshape(-1).astype(np.float32)
    c_idx = c_idx.reshape(-1).astype(np.float32)
    n_freq = 16
    inv = (np.float32(10000.0) ** (-np.arange(n_freq, dtype=np.float32) / np.float32(n_freq))).astype(np.float32)
    ang = np.concatenate([r_idx[:, None] * inv, c_idx[:, None] * inv], axis=-1).astype(np.float32)
    cos = np.cos(ang).astype(np.float32)
    sin = np.sin(ang).astype(np.float32)
    tab = np.zeros((128, 2, SEQ), np.float32)
    for p in range(128):
        d = p % 64
        i = d // 2
        tab[p, 0] = cos[:, i]
        tab[p, 1] = -sin[:, i] if d % 2 == 0 else sin[:, i]
    return tab


def _pool_edge():
    tab = np.zeros((128, 4, 16), np.float32)
    for g, w in enumerate((2, 4, 8, 16)):
        r = w // 2
        for i in range(16):
            t = i if i < 8 else SEQ - 16 + i
            lo = max(t - r, 0)
            hi = min(t + r, SEQ - 1)
            tab[:, g, i] = np.float32(1.0) / np.float32(hi - lo + 1)
    return tab


def prepare_shared(inp):
    f = lambda a: np.ascontiguousarray(np.asarray(a, dtype=np.float32))
    sh = {}
    for k in ("ffn1_w_in", "ffn1_w_out", "ffn2_w_in", "ffn2_w_out", "ev_w_out", "od_w_in", "od_w_out"):
        sh[k] = f(inp[k])
    ev = f(inp["ev_w_in"])
    q = ev[:, :, 1536:2048]
    k = ev[:, :, 2048:2176]
    v = ev[:, :, 2176:2304]
    swap = np.arange(512) ^ 1
    qs = q[:, :, swap]
    ks = k[:, :, np.arange(128) ^ 1]
    kd0 = np.concatenate([k[:, :, 0:64], k[:, :, 0:64]], -1)
    kd1 = np.concatenate([k[:, :, 64:128], k[:, :, 64:128]], -1)
    ksd0 = np.concatenate([ks[:, :, 0:64], ks[:, :, 0:64]], -1)
    ksd1 = np.concatenate([ks[:, :, 64:128], ks[:, :, 64:128]], -1)
    sh["ev_w_in"] = np.ascontiguousarray(np.concatenate([ev[:, :, 0:1536], q, qs, kd0, kd1, ksd0, ksd1, v], -1))
    assert sh["ev_w_in"].shape[-1] == EV_COLS
    sh["od_pool_w"] = np.ascontiguousarray(f(inp["od_pool_w"]).transpose(0, 2, 1, 3))
    sh["od_sgu_wt"] = np.ascontiguousarray(f(inp["od_sgu_w"]).transpose(0, 3, 1, 2))
    par = np.zeros((128, NPAR), np.float32)
    for l in range(4):
        for w, nm in enumerate(("ffn1_norm", "mix_norm", "ffn2_norm")):
            par[:, PC_NORM(l, w):PC_NORM(l, w) + 8] = f(inp[nm])[l].reshape(8, 128).T
    cw = f(inp["ev_conv_w"])
    qn = f(inp["ev_q_norm"])
    kn = f(inp["ev_k_norm"])
    idx = np.arange(128) % 64
    for j in range(2):
        for s_ in range(3):
            par[:, PC_CONV + j * 12 + s_ * 4:PC_CONV + j * 12 + s_ * 4 + 4] = cw[j, s_].reshape(4, 128).T
        par[:, PC_QK + j * 4 + 0] = qn[j][idx]
        par[:, PC_QK + j * 4 + 1] = qn[j][idx ^ 1]
        par[:, PC_QK + j * 4 + 2] = kn[j][idx]
        par[:, PC_QK + j * 4 + 3] = kn[j][idx ^ 1]
        par[:, PC_PSC + j * 4:PC_PSC + j * 4 + 4] = f(inp["od_pool_scale"])[j].reshape(4, 128).T
    sh["params"] = par
    sh["rope"] = _rope_tables()
    sh["fin_g"] = np.ascontiguousarray(np.broadcast_to(f(inp["final_norm"])[None, :], (128, D)))
    sh["sgu_g"] = np.ascontiguousarray(np.broadcast_to(f(inp["od_sgu_norm"])[:, None, :], (2, 128, 512)))
    sb_ = f(inp["od_sgu_b"])
    sh["sgu_b"] = np.ascontiguousarray(np.broadcast_to(sb_[:, None, :, None, :], (2, 128, 4, 4, 128)).reshape(2, 128, 4, 512))
    sh["pool_edge"] = _pool_edge()
    cst = np.zeros((128, 3, 128), np.float32)
    cst[:, 0, :] = np.eye(128, dtype=np.float32)
    cst[:, 1, :] = 1.0
    cst[:, 2, :] = (np.arange(128)[:, None] // 64 == np.arange(128)[None, :] // 64).astype(np.float32)
    sh["consts"] = cst
    return sh


_CACHE = {}


def kernel(**inputs):
    cfg = default_cfg()
    x = np.asarray(inputs["x"], dtype=np.float32)
    B = x.shape[0]
    per = B // N_CORES
    sh = prepare_shared(inputs)
    if "nc" not in _CACHE:
        _CACHE["nc"] = build_program(cfg)[0]
    nc = _CACHE["nc"]
    in_maps = []
    for c in range(N_CORES):
        m = dict(sh)
        m["x"] = np.ascontiguousarray(x[c * per:(c + 1) * per].reshape(per * SEQ, D))
        in_maps.append(m)
    res = run_bass_kernel_spmd(nc, in_maps, core_ids=list(range(N_CORES)))
    out = np.stack([np.asarray(r["out"]).reshape(per, SEQ, D) for r in res.results], 0).reshape(B, SEQ, D)
    return out.astype(np.float32)
```

```python
import numpy as np
from contextlib import ExitStack
import concourse.bass as bass
import concourse.mybir as mybir
from concourse.bass_utils import run_bass_kernel_spmd

F32 = mybir.dt.float32
BF16 = mybir.dt.bfloat16
ALU = mybir.AluOpType
AF = mybir.ActivationFunctionType
AX = mybir.AxisListType

D = 1024
SEQ = 2048
DFF = 2816
NJ = 22
TW = 512
NT = SEQ // TW
PAD = 8
HW = SEQ + 2 * PAD
EPS = 1e-6
N_CORES = 8
EV_COLS = 3200
THIRDS = [(0, 8), (8, 8), (16, 6)]

def PC_NORM(l, w):
    return (l * 3 + w) * 8
PC_CONV = 96
PC_QK = 120
PC_PSC = 128
NPAR = 136

EPOCH = 4096
KDMA = 8


class Op:
    __slots__ = ("eng", "fn", "is_dma", "deps", "signal", "cidx", "qidx")

    def __init__(self, eng, fn, is_dma):
        self.eng = eng
        self.fn = fn
        self.is_dma = is_dma
        self.deps = []
        self.signal = False
        self.cidx = -1
        self.qidx = -1


class Buf:
    __slots__ = ("name", "w", "r")

    def __init__(self, name):
        self.name = name
        self.w = None
        self.r = {}


class Sched:
    ENGS = ("pe", "act", "dve", "pool", "sp")

    def __init__(self):
        self.ops = {e: [] for e in self.ENGS}
        self.nops = 0
        self._bar_pos = {}

    def add(self, eng, fn, reads=(), writes=(), dma=False):
        op = Op(eng, fn, dma)
        deps = {}
        for b in reads:
            if b.w is not None:
                deps[id(b.w)] = b.w
        for b in writes:
            if b.w is not None:
                deps[id(b.w)] = b.w
            for o in b.r.values():
                deps[id(o)] = o
        for d in deps.values():
            if d.eng == "pe" and eng == "pe" and not d.is_dma and not dma:
                continue
            op.deps.append(d)
            d.signal = True
        for b in reads:
            key = (eng, len(b.r)) if dma else eng
            b.r[key] = op
        for b in writes:
            b.w = op
            b.r = {}
        self.ops[eng].append(op)
        self.nops += 1
        return op

    def barrier(self, engs=None):
        lasts = []
        for eng in self.ENGS:
            ops = self.ops[eng]
            start = self._bar_pos.get(eng, 0)
            last_c = None
            for op in ops[start:]:
                if op.fn is None:
                    continue
                if op.is_dma:
                    lasts.append(op)
                else:
                    last_c = op
            if last_c is None:
                for op in reversed(ops[:start]):
                    if op.fn is not None and not op.is_dma:
                        last_c = op
                        break
            if last_c is not None:
                lasts.append(last_c)
        for eng in self.ENGS:
            self._bar_pos[eng] = len(self.ops[eng])
        for eng in (engs or self.ENGS):
            op = Op(eng, None, False)
            op.deps = list(lasts)
            for d in lasts:
                d.signal = True
            self.ops[eng].append(op)

    def emit(self, nc, stack, block):
        csems = {}
        dsems = {}
        for eng in self.ENGS:
            c = 0
            q = 0
            for op in self.ops[eng]:
                if op.is_dma:
                    op.qidx = q
                    q += 1
                elif op.signal:
                    op.cidx = c
                    c += 1
            csems[eng] = [stack.enter_context(nc.semaphore(f"c_{eng}_{i}")) for i in range((c + EPOCH - 1) // EPOCH)]
            dsems[eng] = [stack.enter_context(nc.semaphore(f"d_{eng}_{i}")) for i in range(min(q, KDMA))]

        def sig(op):
            if op.is_dma:
                return dsems[op.eng][op.qidx % KDMA], 16 * (op.qidx // KDMA + 1)
            return csems[op.eng][op.cidx // EPOCH], op.cidx % EPOCH + 1

        def run(engname, e):
            seen = {}

            def wait(s, v):
                k = id(s)
                if seen.get(k, 0) < v:
                    e.wait_ge(s, v)
                    seen[k] = v

            for op in self.ops[engname]:
                need = {}
                for d in op.deps:
                    sd_, vd_ = sig(d)
                    if need.get(id(sd_), (None, 0))[1] < vd_:
                        need[id(sd_)] = (sd_, vd_)
                for sd_, vd_ in need.values():
                    wait(sd_, vd_)
                if op.is_dma:
                    s, v = sig(op)
                    if v > 16:
                        wait(s, v - 16)
                if op.fn is None:
                    continue
                inst = op.fn(e)
                if op.is_dma:
                    inst.then_inc(s, 16)
                elif op.signal:
                    s, v = sig(op)
                    inst.then_inc(s, 1)

        block.tensor(lambda e: run("pe", e))
        block.scalar(lambda e: run("act", e))
        block.vector(lambda e: run("dve", e))
        block.gpsimd(lambda e: run("pool", e))
        block.sync(lambda e: run("sp", e))


def default_cfg():
    return dict(nseq=2, layers=[0, 1, 2, 3], phases=("ffn1", "mix", "ffn2"))


def build_program(cfg):
    nseq = cfg["nseq"]
    layers = cfg["layers"]
    phases = cfg["phases"]
    ntok = nseq * SEQ

    nc = bass.Bass("TRN2", target_bir_lowering=False)
    dr = {}

    def din(name, shape):
        dr[name] = nc.dram_tensor(name, list(shape), F32, kind="ExternalInput").ap()
        return dr[name]

    x_d = din("x", [ntok, D])
    out_d = nc.dram_tensor("out", [ntok, D], F32, kind="ExternalOutput").ap()
    w_in_d = {1: din("ffn1_w_in", [4, D, 2 * DFF]), 2: din("ffn2_w_in", [4, D, 2 * DFF])}
    w_out_d = {1: din("ffn1_w_out", [4, DFF, D]), 2: din("ffn2_w_out", [4, DFF, D])}
    ev_w_in_d = din("ev_w_in", [2, D, EV_COLS])
    ev_w_out_d = din("ev_w_out", [2, D, D])
    od_w_in_d = din("od_w_in", [2, D, 1536])
    od_w_out_d = din("od_w_out", [2, D, D])
    pool_w_d = din("od_pool_w", [2, 128, 4, 128])
    sgu_wt_d = din("od_sgu_wt", [2, 128, 4, 128])
    par_d = din("params", [128, NPAR])
    rope_d = din("rope", [128, 2, SEQ])
    fin_g_d = din("fin_g", [128, D])
    sgu_g_d = din("sgu_g", [2, 128, 512])
    sgu_b_d = din("sgu_b", [2, 128, 4, 128])
    edge_d = din("pool_edge", [128, 4, 16])
    cst_d = din("consts", [128, 3, 128])

    S = Sched()
    stack = ExitStack()
    with stack:
        def sb(name, shape, dt):
            return stack.enter_context(nc.sbuf_tensor(name, shape, dt))

        x_sb = sb("x_sb", [128, 8, SEQ], F32)
        ident = sb("ident", [128, 128], F32)
        cbf = sb("cbf", [128, 2, 128], BF16)
        par = sb("par", [128, NPAR], F32)
        slotA = sb("slotA", [128, 3, 8, 512], BF16)
        regB = sb("regB", [128, 8, 1024], BF16)
        ARENA_BYTES = 102784
        arena = sb("arena", [128, ARENA_BYTES // 4], F32)
        psum = stack.enter_context(nc.psum_tensor("psum", [128, 8, 512], F32))

        ones_bf = cbf[:, 0, :]
        blk_bf = cbf[:, 1, :]

        def carve(off, nbytes, dt):
            assert off % 4 == 0 and nbytes % 4 == 0 and off + nbytes <= ARENA_BYTES, (off, nbytes)
            v = arena[:, off // 4:(off + nbytes) // 4]
            if dt != F32:
                v = v.bitcast(dt)
            return v

        off = 0
        h_sb = carve(off, 8 * HW * 2, BF16).rearrange("p (k t) -> p k t", k=8)
        off += 8 * HW * 2
        sq_sb = [carve(off + i * 1024, 1024, BF16) for i in range(4)]
        off += 4096
        sd_sb = [carve(off + i * 2048, 2048, F32) for i in range(2)]
        off += 4096
        rstd_sb = [carve(off + i * 2048, 2048, F32) for i in range(2)]
        off += 4096
        PH0 = off

        xB = [[Buf(f"x{m}_{t}") for t in range(NT)] for m in range(8)]
        hB = [[Buf(f"h{k}_{t}") for t in range(NT)] for k in range(8)]
        hpadB = Buf("hpad")
        sqB = [Buf(f"sq{i}") for i in range(4)]
        sdB = [Buf(f"sd{i}") for i in range(2)]
        rstdB = [Buf(f"rstd{i}") for i in range(2)]
        bankB = [Buf(f"bank{i}") for i in range(8)]
        slotB = [Buf(f"slot{i}") for i in range(3)]
        regBB = Buf("regB")
        cstB = Buf("cst")
        parB = Buf("par")
        ctr = {"sq": 0, "sd": 0, "slot": 0}

        def bank(i):
            return psum[:, i, :]

        S.add("sp", lambda e: e.dma_start(out=ident[:], in_=cst_d[:, 0, :]), writes=[cstB], dma=True)
        cst2B = Buf("cst2")
        S.add("pool", lambda e: e.dma_start(out=cbf[:], in_=cst_d[:, 1:3, :]), writes=[cst2B], dma=True)
        S.add("sp", lambda e: e.dma_start(out=par[:], in_=par_d), writes=[parB], dma=True)

        def pcol(c):
            return par[:, c:c + 1]

        def load_slab(w2d, c0, ncols):
            s = ctr["slot"] % 3
            ctr["slot"] += 1
            src = w2d.rearrange("(k p) c -> p k c", p=128)[:, :, c0:c0 + ncols]
            dst = slotA[:, s, :, 0:ncols]
            S.add("pool", lambda e: e.dma_start(out=dst, in_=src), writes=[slotB[s]], dma=True)
            return s

        def load_regB(w2d, r0, nchunks):
            src = w2d[r0 * 128:(r0 + nchunks) * 128, :].rearrange("(j p) c -> p j c", p=128)
            dst = regB[:, 0:nchunks, :]
            S.add("pool", lambda e: e.dma_start(out=dst, in_=src), writes=[regBB], dma=True)

        xin = [carve(PH0 + i * 4096, 4096, F32) for i in range(2)]
        xinB = [Buf(f"xin{i}") for i in range(2)]

        def load_x(s):
            for tt in range(SEQ // 128):
                sl = tt % 2
                r0 = s * SEQ + tt * 128
                S.add("sp", lambda e, sl=sl, r0=r0: e.dma_start(out=xin[sl], in_=x_d[r0:r0 + 128, :]),
                      writes=[xinB[sl]], dma=True)
                t = tt // 4
                for half in range(2):
                    b = (tt * 2 + half) % 8
                    for mm_ in range(4):
                        m = half * 4 + mm_
                        S.add("pe", lambda e, b=b, mm_=mm_, m=m, sl=sl: e.transpose(
                            psum[:, b, mm_ * 128:(mm_ + 1) * 128], xin[sl][:, m * 128:(m + 1) * 128], ident[:]),
                            reads=[xinB[sl], cstB], writes=[bankB[b]])
                    dst = x_sb[:, half * 4:half * 4 + 4, tt * 128:(tt + 1) * 128]
                    srcp = psum[:, b, :].rearrange("p (a c) -> p a c", a=4)
                    eng = "act" if half == 0 else "dve"
                    if eng == "act":
                        S.add("act", lambda e, dst=dst, srcp=srcp: e.activation(out=dst, in_=srcp, func=AF.Copy),
                              reads=[bankB[b]], writes=[xB[half * 4 + i][t] for i in range(4)])
                    else:
                        S.add("dve", lambda e, dst=dst, srcp=srcp: e.tensor_copy(out=dst, in_=srcp),
                              reads=[bankB[b]], writes=[xB[half * 4 + i][t] for i in range(4)])

        fin_g = carve(PH0 + 8192, 4096, F32)
        fin_gB = Buf("fin_g")
        ot_sb = [carve(PH0 + 12288 + i * 4096, 4096, F32) for i in range(2)]
        otB = [Buf(f"ot{i}") for i in range(2)]
        fsq = carve(PH0 + 20480, 4096, F32)
        fsqB = Buf("fsq")
        fss = [carve(PH0 + 24576 + i * 16, 4, F32) for i in range(4)]
        fssB = [Buf(f"fss{i}") for i in range(4)]
        out_ops = []

        def final_store(s):
            S.add("sp", lambda e: e.dma_start(out=fin_g, in_=fin_g_d), writes=[fin_gB], dma=True)
            for tt in range(SEQ // 128):
                t = tt // 4
                pb = (tt % 4) * 2
                for m in range(8):
                    b = pb + m // 4
                    S.add("pe", lambda e, b=b, m=m, tt=tt: e.transpose(
                        psum[:, b, (m % 4) * 128:(m % 4 + 1) * 128], x_sb[:, m, tt * 128:(tt + 1) * 128], ident[:]),
                        reads=[xB[m][t], cstB], writes=[bankB[b]])
                pv = psum[:, pb:pb + 2, :].rearrange("p a c -> p (a c)")
                i2 = tt % 2
                ss, ssB = fss[i2], fssB[i2]
                sd1, sd1B = fss[2 + i2], fssB[2 + i2]
                S.add("act", lambda e, pv=pv: e.activation(out=fsq, in_=pv, func=AF.Square),
                      reads=[bankB[pb], bankB[pb + 1]], writes=[fsqB])
                S.add("dve", lambda e, ss=ss: e.reduce_sum(out=ss, in_=fsq, axis=AX.X), reads=[fsqB], writes=[ssB])
                S.add("act", lambda e, ss=ss, sd1=sd1: e.activation(out=sd1, in_=ss, func=AF.Sqrt, bias=EPS, scale=1.0 / D),
                      reads=[ssB], writes=[sd1B])
                S.add("dve", lambda e, sd1=sd1: e.reciprocal(out=sd1, in_=sd1), reads=[sd1B], writes=[sd1B])
                o = ot_sb[i2]
                S.add("dve", lambda e, o=o, pv=pv, sd1=sd1: e.scalar_tensor_tensor(
                    out=o, in0=pv, scalar=sd1, in1=fin_g, op0=ALU.mult, op1=ALU.mult),
                    reads=[bankB[pb], bankB[pb + 1], sd1B, fin_gB], writes=[otB[i2]])
                r0 = s * SEQ + tt * 128
                out_ops.append(S.add("sp", lambda e, o=o, r0=r0: e.dma_start(out=out_d[r0:r0 + 128, :], in_=o),
                                     reads=[otB[i2]], dma=True))

        def zero_hpad():
            S.add("pool", lambda e: e.memset(h_sb[:, :, 0:PAD], 0.0), writes=[hpadB])
            S.add("pool", lambda e: e.memset(h_sb[:, :, PAD + SEQ:HW], 0.0), writes=[hpadB])

        def norm_phase(gcol0):
            zero_hpad()
            for t in range(NT):
                c0 = t * TW
                nb = t % 2
                for m in range(8):
                    i = ctr["sq"] % 4
                    ctr["sq"] += 1
                    S.add("act", lambda e, i=i, m=m, c0=c0: e.activation(out=sq_sb[i], in_=x_sb[:, m, c0:c0 + TW], func=AF.Square),
                          reads=[xB[m][t]], writes=[sqB[i]])
                    S.add("pe", lambda e, i=i, m=m, nb=nb: e.matmul(bank(nb), lhsT=ones_bf, rhs=sq_sb[i], start=(m == 0), stop=(m == 7)),
                          reads=[sqB[i], cst2B], writes=[bankB[nb]])
                i = ctr["sd"] % 2
                ctr["sd"] += 1
                S.add("act", lambda e, i=i, nb=nb: e.activation(out=sd_sb[i], in_=bank(nb), func=AF.Sqrt, bias=EPS, scale=1.0 / D),
                      reads=[bankB[nb]], writes=[sdB[i]])
                S.add("dve", lambda e, i=i: e.reciprocal(out=rstd_sb[i], in_=sd_sb[i]), reads=[sdB[i]], writes=[rstdB[i]])
                for m in range(8):
                    S.add("dve", lambda e, i=i, m=m, c0=c0: e.scalar_tensor_tensor(
                        out=h_sb[:, m, PAD + c0:PAD + c0 + TW], in0=x_sb[:, m, c0:c0 + TW], scalar=pcol(gcol0 + m),
                        in1=rstd_sb[i], op0=ALU.mult, op1=ALU.mult),
                        reads=[xB[m][t], rstdB[i], parB], writes=[hB[m][t]])

        def hcols(k, t):
            return h_sb[:, k, PAD + t * TW:PAD + (t + 1) * TW]

        act_sb = carve(PH0, 8 * SEQ * 2, BF16).rearrange("p (j t) -> p j t", j=8)
        sg_sb = [carve(PH0 + 8 * SEQ * 2 + i * 2048, 2048, F32) for i in range(3)]
        actB = [[Buf(f"act{j}_{t}") for t in range(NT)] for j in range(8)]
        sgB = [Buf(f"sg{i}") for i in range(3)]
        ctr["sg"] = 0
        ctr["gu"] = 0

        def ffn_phase(l, which):
            w_in = w_in_d[which][l]
            w_out = w_out_d[which][l]
            w_in_v = w_in.rearrange("(k p) c -> p k c", p=128)
            pairs = [(ti, j0 + 2 * pi) for ti, (j0, nj) in enumerate(THIRDS) for pi in range(nj // 2)]
            slots = {}

            def issue(gi):
                if gi >= len(pairs):
                    return
                s = ctr["slot"] % 3
                ctr["slot"] += 1
                jj = pairs[gi][1]
                srcg = w_in_v[:, :, jj * 128:jj * 128 + 256]
                srcu = w_in_v[:, :, DFF + jj * 128:DFF + jj * 128 + 256]
                S.add("pool", lambda e: e.dma_start(out=slotA[:, s, :, 0:256], in_=srcg), writes=[slotB[s]], dma=True)
                S.add("pool", lambda e: e.dma_start(out=slotA[:, s, :, 256:512], in_=srcu), writes=[slotUB[s]], dma=True)
                slots[gi] = s

            def prefetch():
                issue(0)
                issue(1)

            def body():
                ffn_body(l, which, w_out, slots, issue)

            return prefetch, body

        def ffn_body(l, which, w_out, slots, issue):
            norm_phase(PC_NORM(l, 0 if which == 1 else 2))
            gi = 0
            for ti, (j0, nj) in enumerate(THIRDS):
                npairs = nj // 2
                load_regB(w_out, j0, nj)
                for pi in range(npairs):
                    s = slots[gi]
                    issue(gi + 2)
                    gi += 1
                    for t in range(NT):
                        for jj in range(2):
                            jl = 2 * pi + jj
                            gb = (ctr["gu"] % 3) * 2 + 2
                            ub = gb + 1
                            ctr["gu"] += 1
                            for k in range(8):
                                S.add("pe", lambda e, s=s, k=k, jj=jj, gb=gb, t=t: e.matmul(
                                    bank(gb), lhsT=slotA[:, s, k, jj * 128:(jj + 1) * 128], rhs=hcols(k, t), start=(k == 0), stop=(k == 7)),
                                    reads=[slotB[s], hB[k][t]], writes=[bankB[gb]])
                            for k in range(8):
                                S.add("pe", lambda e, s=s, k=k, jj=jj, ub=ub, t=t: e.matmul(
                                    bank(ub), lhsT=slotA[:, s, k, 256 + jj * 128:256 + (jj + 1) * 128], rhs=hcols(k, t), start=(k == 0), stop=(k == 7)),
                                    reads=[slotUB[s], hB[k][t]], writes=[bankB[ub]])
                            i = ctr["sg"] % 3
                            ctr["sg"] += 1
                            S.add("act", lambda e, i=i, gb=gb: e.activation(out=sg_sb[i], in_=bank(gb), func=AF.Silu),
                                  reads=[bankB[gb]], writes=[sgB[i]])
                            S.add("dve", lambda e, i=i, ub=ub, jl=jl, t=t: e.tensor_tensor(
                                out=act_sb[:, jl, t * TW:(t + 1) * TW], in0=sg_sb[i], in1=bank(ub), op=ALU.mult),
                                reads=[sgB[i], bankB[ub]], writes=[actB[jl][t]])
                for t in range(NT):
                    for m in range(8):
                        ob = m % 2
                        for jl in range(nj):
                            S.add("pe", lambda e, ob=ob, jl=jl, m=m, t=t, nj=nj: e.matmul(
                                bank(ob), lhsT=regB[:, jl, m * 128:(m + 1) * 128], rhs=act_sb[:, jl, t * TW:(t + 1) * TW],
                                start=(jl == 0), stop=(jl == nj - 1)),
                                reads=[regBB, actB[jl][t]], writes=[bankB[ob]])
                        S.add("dve", lambda e, ob=ob, m=m, t=t: e.scalar_tensor_tensor(
                            out=x_sb[:, m, t * TW:(t + 1) * TW], in0=bank(ob), scalar=0.5, in1=x_sb[:, m, t * TW:(t + 1) * TW],
                            op0=ALU.mult, op1=ALU.add),
                            reads=[bankB[ob], xB[m][t]], writes=[xB[m][t]])

        TB = [Buf(f"T{i}") for i in range(4)]
        ctr["T"] = 0
        ctr["ss"] = 0

        def issue_slab(w2d, c0, ncols):
            s = ctr["slot"] % 3
            ctr["slot"] += 1
            src = w2d.rearrange("(k p) c -> p k c", p=128)[:, :, c0:c0 + ncols]
            S.add("pool", lambda e: e.dma_start(out=slotA[:, s, :, 0:ncols], in_=src), writes=[slotB[s], slotUB[s]], dma=True)
            return s

        def proj(bk, s, col0, t, ncol=128):
            for k in range(8):
                S.add("pe", lambda e, k=k: e.matmul(bank(bk), lhsT=slotA[:, s, k, col0:col0 + ncol], rhs=hcols(k, t),
                                                   start=(k == 0), stop=(k == 7)),
                      reads=[slotB[s], hB[k][t]], writes=[bankB[bk]])

        def out_proj(cat, catB, t, banks):
            for m in range(8):
                ob = banks[m % len(banks)]
                for c in range(8):
                    S.add("pe", lambda e, ob=ob, c=c, m=m: e.matmul(
                        bank(ob), lhsT=regB[:, c, m * 128:(m + 1) * 128], rhs=cat[:, c, :], start=(c == 0), stop=(c == 7)),
                        reads=[regBB, catB[c]], writes=[bankB[ob]])
                S.add("dve", lambda e, ob=ob, m=m: e.tensor_tensor(
                    out=x_sb[:, m, t * TW:(t + 1) * TW], in0=bank(ob), in1=x_sb[:, m, t * TW:(t + 1) * TW], op=ALU.add),
                    reads=[bankB[ob], xB[m][t]], writes=[xB[m][t]])

        o = PH0
        kT_sb = carve(o, 2 * SEQ * 2, BF16).rearrange("p (c t) -> p c t", c=2); o += 2 * SEQ * 2
        vaug = carve(o, 16 * 2 * 192 * 2, BF16).rearrange("p (a b c) -> p a b c", a=16, b=2); o += 16 * 2 * 192 * 2
        rope_sb = [carve(o + i * 4096, 4096, F32).rearrange("p (a t) -> p a t", a=2) for i in range(2)]; o += 8192
        qT_sb = carve(o, 4 * TW * 2, BF16).rearrange("p (c t) -> p c t", c=4); o += 4 * TW * 2
        cat_e = carve(o, 8 * TW * 2, BF16).rearrange("p (c t) -> p c t", c=8); o += 8 * TW * 2
        ybuf = [carve(o + i * 2064, 2064, F32) for i in range(2)]; o += 4128
        halo_sb = carve(o, 64, F32); o += 64
        PT2 = [carve(o + i * 2048, 2048, BF16).rearrange("p (a t) -> p a t", a=2) for i in range(2)]; o += 4096
        T_e = [carve(o + i * 2048, 2048, F32) for i in range(4)]; o += 8192
        assert o <= ARENA_BYTES, o
        kTB = [[Buf(f"kT{c}_{t}") for t in range(NT)] for c in range(2)]
        vaugB = [Buf(f"vaug{t}") for t in range(NT)]
        vonesB = Buf("vones")
        ropeB = [Buf(f"rope{i}") for i in range(2)]
        qTB = [Buf(f"qT{c}") for c in range(4)]
        catB = [Buf(f"cat{c}") for c in range(8)]
        ybufB = [Buf(f"ybuf{i}") for i in range(2)]
        haloB = Buf("halo")
        PTB = [Buf(f"PT{i}") for i in range(2)]
        ctr["rope"] = 0
        ctr["pt"] = 0
        ctr["yb"] = 0
        ctr["pair"] = 0
        ctr["tp"] = 0
        ctr["sp"] = 0

        def next_pair():
            gb = (ctr["pair"] % 3) * 2 + 2
            ctr["pair"] += 1
            return gb, gb + 1

        def rope_apply(T, pb, sbk, g, gs, rp, out_ap, outB):
            i = ctr["sq"] % 4
            ctr["sq"] += 1
            S.add("act", lambda e: e.activation(out=sq_sb[i], in_=bank(pb), func=AF.Square), reads=[bankB[pb]], writes=[sqB[i]])
            S.add("pe", lambda e: e.matmul(bank(0), lhsT=blk_bf, rhs=sq_sb[i], start=True, stop=True),
                  reads=[sqB[i], cst2B], writes=[bankB[0]])
            d = ctr["sd"] % 2
            ctr["sd"] += 1
            S.add("act", lambda e: e.activation(out=sd_sb[d], in_=bank(0), func=AF.Sqrt, bias=EPS, scale=1.0 / 64),
                  reads=[bankB[0]], writes=[sdB[d]])
            S.add("dve", lambda e: e.reciprocal(out=rstd_sb[d], in_=sd_sb[d]), reads=[sdB[d]], writes=[rstdB[d]])
            tp = ctr["tp"] % 2
            ctr["tp"] += 1
            t0_, t1_ = 2 * tp, 2 * tp + 1
            S.add("dve", lambda e: e.scalar_tensor_tensor(out=T[t0_], in0=bank(pb), scalar=g, in1=rope_sb[rp][:, 0, :],
                                                           op0=ALU.mult, op1=ALU.mult),
                  reads=[bankB[pb], parB, ropeB[rp]], writes=[TB[t0_]])
            S.add("dve", lambda e: e.scalar_tensor_tensor(out=T[t1_], in0=bank(sbk), scalar=gs, in1=rope_sb[rp][:, 1, :],
                                                           op0=ALU.mult, op1=ALU.mult),
                  reads=[bankB[sbk], parB, ropeB[rp]], writes=[TB[t1_]])
            S.add("dve", lambda e: e.tensor_tensor(out=T[t0_], in0=T[t0_], in1=T[t1_], op=ALU.add), reads=[TB[t0_], TB[t1_]], writes=[TB[t0_]])
            S.add("dve", lambda e: e.tensor_tensor(out=out_ap, in0=T[t0_], in1=rstd_sb[d], op=ALU.mult),
                  reads=[TB[t0_], rstdB[d]], writes=[outB])

        def load_rope(t):
            rp = ctr["rope"] % 2
            ctr["rope"] += 1
            S.add("sp", lambda e: e.dma_start(out=rope_sb[rp], in_=rope_d[:, :, t * TW:(t + 1) * TW]), writes=[ropeB[rp]], dma=True)
            return rp

        def even_mixer(l):
            j = l // 2
            w_in = ev_w_in_d[j]
            gq, gqs, gk, gks = [pcol(PC_QK + j * 4 + i) for i in range(4)]
            T = T_e
            st = {}

            def prefetch():
                st["sK"] = issue_slab(w_in, 20 * 128, 512)
                st["sV"] = issue_slab(w_in, 24 * 128, 128)
                load_regB(ev_w_out_d[j], 0, 8)

            def body():
                even_body(l, j, w_in, gq, gqs, gk, gks, T, st["sK"], st["sV"])

            return prefetch, body

        def even_body(l, j, w_in, gq, gqs, gk, gks, T, sK, sV):
            S.add("pool", lambda e: e.memset(vaug[:, :, :, 0:64], 1.0), writes=[vonesB])
            S.add("pool", lambda e: e.memset(vaug[:, :, :, 128:192], 1.0), writes=[vonesB])
            norm_phase(PC_NORM(l, 1))
            for t in range(NT):
                rp = load_rope(t)
                for kc in range(2):
                    gb, ub = next_pair()
                    proj(gb, sK, kc * 128, t)
                    proj(ub, sK, (2 + kc) * 128, t)
                    rope_apply(T, gb, ub, gk, gks, rp, kT_sb[:, kc, t * TW:(t + 1) * TW], kTB[kc][t])
                for sub in range(4):
                    for k in range(8):
                        S.add("pe", lambda e, k=k, sub=sub, t=t: e.matmul(
                            psum[:, 1, sub * 128:(sub + 1) * 128],
                            lhsT=h_sb[:, k, PAD + t * TW + sub * 128:PAD + t * TW + (sub + 1) * 128],
                            rhs=slotA[:, sV, k, 0:128], start=(k == 0), stop=(k == 7)),
                            reads=[slotB[sV], hB[k][t]], writes=[bankB[1]])
                for kv in range(2):
                    srcv = psum[:, 1, :].rearrange("p (s c) -> p s c", s=4)[:, :, kv * 64:(kv + 1) * 64]
                    dstv = vaug[:, t * 4:(t + 1) * 4, kv, 64:128]
                    S.add("act", lambda e, srcv=srcv, dstv=dstv: e.activation(out=dstv, in_=srcv, func=AF.Copy),
                          reads=[bankB[1]], writes=[vaugB[t]])
            sQ = issue_slab(w_in, 12 * 128, 512)
            sQS = issue_slab(w_in, 16 * 128, 512)
            sC = issue_slab(w_in, 4 * 128, 512)
            for t in range(NT):
                rp = load_rope(t)
                for qc in range(4):
                    gb, ub = next_pair()
                    proj(gb, sQ, qc * 128, t)
                    proj(ub, sQS, qc * 128, t)
                    rope_apply(T, gb, ub, gq, gqs, rp, qT_sb[:, qc, :], qTB[qc])
                sH = issue_slab(w_in, 8 * 128, 512)
                sBg = issue_slab(w_in, 0, 512)
                a0 = PAD + t * TW - 1
                for c in range(4):
                    gb, ub = next_pair()
                    proj(gb, sC, c * 128, t)
                    proj(ub, sH, c * 128, t)
                    for (sl_, hc0) in ((sC, c * 4), (sH, c * 4 + 2)):
                        for k in range(8):
                            S.add("pe", lambda e, k=k, sl_=sl_, hc0=hc0, c=c, a0=a0: e.matmul(
                                psum[:, 1, hc0:hc0 + 2], lhsT=slotA[:, sl_, k, c * 128:(c + 1) * 128],
                                rhs=h_sb[:, k, a0:a0 + 514:513], start=(k == 0), stop=(k == 7)),
                                reads=[slotB[sl_], hB[k][t], hpadB] + ([hB[k][t - 1]] if t > 0 else []) + ([hB[k][t + 1]] if t < NT - 1 else []),
                                writes=[bankB[1]])
                    yi = ctr["yb"] % 2
                    ctr["yb"] += 1
                    yb = ybuf[yi]
                    S.add("act", lambda e, gb=gb: e.activation(out=T[2], in_=bank(gb), func=AF.Copy), reads=[bankB[gb]], writes=[TB[2]])
                    S.add("dve", lambda e, ub=ub, yb=yb: e.tensor_tensor(out=yb[:, 1:513], in0=T[2], in1=bank(ub), op=ALU.mult),
                          reads=[TB[2], bankB[ub]], writes=[ybufB[yi]])
                    S.add("act", lambda e, c=c: e.activation(out=halo_sb[:, c * 4:c * 4 + 2], in_=psum[:, 1, c * 4:c * 4 + 2], func=AF.Copy),
                          reads=[bankB[1]], writes=[haloB])
                    S.add("dve", lambda e, c=c, yb=yb: e.tensor_tensor(out=yb[:, 0:514:513], in0=halo_sb[:, c * 4:c * 4 + 2],
                                                                         in1=psum[:, 1, c * 4 + 2:c * 4 + 4], op=ALU.mult),
                          reads=[haloB, bankB[1], ybufB[yi]], writes=[ybufB[yi]])
                    w0, w1, w2 = [pcol(PC_CONV + j * 12 + s_ * 4 + c) for s_ in range(3)]
                    S.add("act", lambda e, yb=yb, w0=w0: e.activation(out=T[3], in_=yb[:, 0:512], func=AF.Copy, scale=w0),
                          reads=[ybufB[yi], parB], writes=[TB[3]])
                    S.add("dve", lambda e, yb=yb, w1=w1: e.scalar_tensor_tensor(out=T[3], in0=yb[:, 1:513], scalar=w1, in1=T[3],
                                                                                  op0=ALU.mult, op1=ALU.add),
                          reads=[ybufB[yi], parB, TB[3]], writes=[TB[3]])
                    S.add("dve", lambda e, yb=yb, w2=w2: e.scalar_tensor_tensor(out=T[3], in0=yb[:, 2:514], scalar=w2, in1=T[3],
                                                                                  op0=ALU.mult, op1=ALU.add),
                          reads=[ybufB[yi], parB, TB[3]], writes=[TB[3]])
                    gb2, _ = next_pair()
                    proj(gb2, sBg, c * 128, t)
                    S.add("dve", lambda e, gb2=gb2, c=c: e.tensor_tensor(out=cat_e[:, c, :], in0=T[3], in1=bank(gb2), op=ALU.mult),
                          reads=[TB[3], bankB[gb2]], writes=[catB[c]])
                if t < NT - 1:
                    sQ = issue_slab(w_in, 12 * 128, 512)
                    sQS = issue_slab(w_in, 16 * 128, 512)
                    sC = issue_slab(w_in, 4 * 128, 512)
                for qc in range(4):
                    kv = qc // 2
                    pis = {}

                    def s_batch(kt, qc=qc, kv=kv):
                        pb = 2 + 2 * (ctr["sp"] % 3)
                        ctr["sp"] += 1
                        pi = ctr["pt"] % 2
                        ctr["pt"] += 1
                        S.add("pe", lambda e: e.matmul(bank(pb), lhsT=kT_sb[0:64, kv, kt * 128:(kt + 1) * 128],
                                                       rhs=qT_sb[0:64, qc, :], start=True, stop=True),
                              reads=[kTB[kv][kt // 4], qTB[qc]], writes=[bankB[pb], bankB[pb + 1]])
                        S.add("pe", lambda e: e.matmul(bank(pb + 1), lhsT=kT_sb[64:128, kv, kt * 128:(kt + 1) * 128],
                                                       rhs=qT_sb[64:128, qc, :], start=True, stop=True),
                              reads=[kTB[kv][kt // 4], qTB[qc]], writes=[bankB[pb + 1]])
                        S.add("act", lambda e: e.activation(out=PT2[pi], in_=psum[:, pb:pb + 2, :], func=AF.Exp, scale=0.125),
                              reads=[bankB[pb], bankB[pb + 1]], writes=[PTB[pi]])
                        pis[kt] = pi

                    def pv_batch(kt, kv=kv):
                        pi = pis[kt]
                        S.add("pe", lambda e: e.matmul(bank(0), lhsT=vaug[:, kt, kv, 64:192], rhs=PT2[pi][:, 0, :],
                                                       start=(kt == 0), stop=(kt == 15)),
                              reads=[vaugB[kt // 4], vonesB, PTB[pi]], writes=[bankB[0]])
                        S.add("pe", lambda e: e.matmul(bank(1), lhsT=vaug[:, kt, kv, 0:128], rhs=PT2[pi][:, 1, :],
                                                       start=(kt == 0), stop=(kt == 15)),
                              reads=[vaugB[kt // 4], vonesB, PTB[pi]], writes=[bankB[1]])

                    s_batch(0)
                    s_batch(1)
                    for kt in range(16):
                        pv_batch(kt)
                        if kt + 2 < 16:
                            s_batch(kt + 2)
                    S.add("dve", lambda e: e.reciprocal(out=T[0][64:128, :], in_=bank(0)[64:128, :]), reads=[bankB[0]], writes=[TB[0]])
                    S.add("dve", lambda e, qc=qc: e.tensor_tensor(out=cat_e[0:64, 4 + qc, :], in0=bank(0)[0:64, :], in1=T[0][64:128, :], op=ALU.mult),
                          reads=[bankB[0], TB[0], catB[4 + qc]], writes=[catB[4 + qc]])
                    S.add("dve", lambda e: e.reciprocal(out=T[1][0:64, :], in_=bank(1)[0:64, :]), reads=[bankB[1]], writes=[TB[1]])
                    S.add("dve", lambda e, qc=qc: e.tensor_tensor(out=cat_e[64:128, 4 + qc, :], in0=bank(1)[64:128, :], in1=T[1][0:64, :], op=ALU.mult),
                          reads=[bankB[1], TB[1], catB[4 + qc]], writes=[catB[4 + qc]])
                out_proj(cat_e, catB, t, (2, 3, 4, 5))

        o = PH0
        pbuf = carve(o, 4 * 528 * 4, F32).rearrange("p (g t) -> p g t", g=4); o += 4 * 528 * 4
        gu_sb = carve(o, 4 * 2048, F32).rearrange("p (g t) -> p g t", g=4); o += 4 * 2048
        T_o6 = [carve(o + i * 2112, 2112, F32) for i in range(6)]; o += 6 * 2112
        T_o = [v[:, 0:512] for v in T_o6]
        vn_sb = carve(o, 4096, BF16).rearrange("p (s c) -> p s c", s=4); o += 4096
        pw_sb = [carve(o + i * 2112, 2112, F32) for i in range(2)]; o += 4224
        pw2_sb = [T_o6[0], T_o6[1]]
        pooled_sb = [carve(o + i * 1024, 1024, BF16) for i in range(4)]; o += 4096
        cat_o = carve(o, 8 * TW * 2, BF16).rearrange("p (c t) -> p c t", c=8); o += 8 * TW * 2
        sgug_sb = carve(o, 2048, F32); o += 2048
        sgub_sb = carve(o, 2048, F32).rearrange("p (g t) -> p g t", g=4); o += 2048
        poolw_sb = carve(o, 1024, BF16).rearrange("p (g d) -> p g d", g=4); o += 1024
        wst_sb = carve(o, 1024, BF16).rearrange("p (g d) -> p g d", g=4); o += 1024
        edge_sb = carve(o, 256, F32).rearrange("p (g i) -> p g i", g=4); o += 256
        vss_sb = [carve(o + i * 16, 4, F32) for i in range(4)]; o += 64
        assert o <= ARENA_BYTES, o
        pbufB = [Buf(f"pbuf{g}") for g in range(4)]
        guB = [Buf(f"gu{g}") for g in range(4)]
        vnB = [Buf(f"vn{s}") for s in range(4)]
        pwB = [Buf(f"pw{i}") for i in range(2)]
        pw2B = [TB[0], TB[1]]
        pooledB = [Buf(f"pooled{i}") for i in range(4)]
        TB.extend([Buf("T4"), Buf("T5")])
        otabB = Buf("otab")
        otab2B = Buf("otab2")
        vssB = [Buf(f"vss{i}") for i in range(4)]
        ctr["pl"] = 0
        ctr["vss"] = 0
        GC1 = 0.044715
        GC2 = 1.5957691216057308

        def gelu(T, src_ps, srcB, out_ap, outB, ta, tb):
            S.add("act", lambda e: e.activation(out=T[ta], in_=src_ps, func=AF.Square), reads=[srcB], writes=[TB[ta]])
            S.add("dve", lambda e: e.tensor_scalar(out=T[tb], in0=T[ta], scalar1=GC1, scalar2=1.0, op0=ALU.mult, op1=ALU.add),
                  reads=[TB[ta]], writes=[TB[tb]])
            S.add("dve", lambda e: e.tensor_tensor(out=T[tb], in0=T[tb], in1=src_ps, op=ALU.mult), reads=[TB[tb], srcB], writes=[TB[tb]])
            S.add("act", lambda e: e.activation(out=T[ta], in_=T[tb], func=AF.Sigmoid, scale=GC2), reads=[TB[tb]], writes=[TB[ta]])
            S.add("dve", lambda e: e.tensor_tensor(out=out_ap, in0=T[ta], in1=src_ps, op=ALU.mult), reads=[TB[ta], srcB], writes=[outB])

        def odd_mixer(l):
            j = l // 2
            w_in = od_w_in_d[j]
            T = T_o
            st = {}

            def prefetch():
                st["sP"] = issue_slab(w_in, 0, 512)
                st["sU"] = issue_slab(w_in, 512, 512)
                st["sVv"] = issue_slab(w_in, 1024, 512)
                load_regB(od_w_out_d[j], 0, 8)

            def body():
                odd_body(l, j, T, st["sP"], st["sU"], st["sVv"])

            return prefetch, body

        def odd_body(l, j, T, sP, sU, sVv):
            S.add("sp", lambda e: e.dma_start(out=sgug_sb, in_=sgu_g_d[j]), writes=[otabB], dma=True)
            S.add("sp", lambda e: e.dma_start(out=sgub_sb, in_=sgu_b_d[j]), writes=[otabB], dma=True)
            S.add("sp", lambda e: e.dma_start(out=edge_sb, in_=edge_d), writes=[otabB], dma=True)
            S.add("pool", lambda e: e.dma_start(out=poolw_sb, in_=pool_w_d[j]), writes=[otab2B], dma=True)
            S.add("pool", lambda e: e.dma_start(out=wst_sb, in_=sgu_wt_d[j]), writes=[otab2B], dma=True)
            norm_phase(PC_NORM(l, 1))
            for t in range(NT):
                a = t * TW
                for g in range(4):
                    gb, _ = next_pair()
                    proj(gb, sP, g * 128, t)
                    for (hc0, c0) in ((g * 16, a), (g * 16 + 8, a + 520)):
                        for k in range(8):
                            S.add("pe", lambda e, k=k, hc0=hc0, c0=c0, g=g: e.matmul(
                                psum[:, 1, hc0:hc0 + 8], lhsT=slotA[:, sP, k, g * 128:(g + 1) * 128],
                                rhs=h_sb[:, k, c0:c0 + 8], start=(k == 0), stop=(k == 7)),
                                reads=[slotB[sP], hB[k][t], hpadB] + ([hB[k][t - 1]] if t > 0 else []) + ([hB[k][t + 1]] if t < NT - 1 else []),
                                writes=[bankB[1]])
                    P = pbuf[:, g, :]
                    S.add("act", lambda e, gb=gb, P=P: e.activation(out=P[:, 8:520], in_=bank(gb), func=AF.Copy), reads=[bankB[gb]], writes=[pbufB[g]])
                    S.add("act", lambda e, g=g, P=P: e.activation(out=P[:, 0:8], in_=psum[:, 1, g * 16:g * 16 + 8], func=AF.Copy),
                          reads=[bankB[1], pbufB[g]], writes=[pbufB[g]])
                    S.add("act", lambda e, g=g, P=P: e.activation(out=P[:, 520:528], in_=psum[:, 1, g * 16 + 8:g * 16 + 16], func=AF.Copy),
                          reads=[bankB[1], pbufB[g]], writes=[pbufB[g]])
                for g in range(4):
                    r = (1, 2, 4, 8)[g]
                    P = pbuf[:, g, :]
                    eng = "pool" if g < 2 else "dve"
                    A_, B_ = (pw_sb[0], pw_sb[1])
                    aB, bB = (pwB[0], pwB[1])
                    if g >= 2:
                        A_, B_ = pw2_sb[0], pw2_sb[1]
                        aB, bB = pw2B[0], pw2B[1]
                    wi = 3 if g % 2 == 0 else 5
                    W_, WB_ = T[wi], TB[wi]

                    def padd(out_ap, in0, in1, rB, wB, eng=eng):
                        S.add(eng, lambda e: e.tensor_tensor(out=out_ap, in0=in0, in1=in1, op=ALU.add), reads=rB, writes=wB)

                    padd(A_[:, 0:527], P[:, 0:527], P[:, 1:528], [pbufB[g]], [aB])
                    if r == 1:
                        padd(W_, A_[:, 7:519], P[:, 9:521], [aB, pbufB[g]], [WB_])
                    else:
                        padd(B_[:, 0:525], A_[:, 0:525], A_[:, 2:527], [aB], [bB])
                        if r == 2:
                            padd(W_, B_[:, 6:518], P[:, 10:522], [bB, pbufB[g]], [WB_])
                        else:
                            padd(A_[:, 0:521], B_[:, 0:521], B_[:, 4:525], [bB, aB], [aB])
                            if r == 4:
                                padd(W_, A_[:, 4:516], P[:, 12:524], [aB, pbufB[g]], [WB_])
                            else:
                                padd(B_[:, 0:513], A_[:, 0:513], A_[:, 8:521], [aB, bB], [bB])
                                padd(W_, B_[:, 0:512], P[:, 16:528], [bB, pbufB[g]], [WB_])
                    pl = pooled_sb[g]
                    inv = 1.0 / (2 * r + 1)
                    S.add("dve", lambda e, pl=pl, P=P, inv=inv, W_=W_: e.scalar_tensor_tensor(out=pl, in0=W_, scalar=inv, in1=P[:, 8:520],
                                                                                              op0=ALU.mult, op1=ALU.subtract),
                          reads=[WB_, pbufB[g]], writes=[pooledB[g]])
                    for (cond, c0, e0) in ((t == 0, 0, 0), (t == NT - 1, 504, 8)):
                        if cond:
                            S.add("dve", lambda e, c0=c0, e0=e0, g=g, W_=W_: e.tensor_tensor(out=W_[:, c0:c0 + 8], in0=W_[:, c0:c0 + 8],
                                                                                               in1=edge_sb[:, g, e0:e0 + 8], op=ALU.mult),
                                  reads=[WB_, otabB, pooledB[g]], writes=[WB_])
                            S.add("dve", lambda e, c0=c0, pl=pl, P=P, W_=W_: e.tensor_tensor(out=pl[:, c0:c0 + 8], in0=W_[:, c0:c0 + 8],
                                                                                               in1=P[:, 8 + c0:16 + c0], op=ALU.subtract),
                                  reads=[WB_, pbufB[g], pooledB[g]], writes=[pooledB[g]])
                for g in range(4):
                    gb, _ = next_pair()
                    proj(gb, sU, g * 128, t)
                    S.add("act", lambda e, gb=gb, g=g: e.activation(out=gu_sb[:, g, :], in_=bank(gb), func=AF.Gelu_apprx_tanh),
                          reads=[bankB[gb]], writes=[guB[g]])
                for sub in range(4):
                    vb, _ = next_pair()
                    for k in range(8):
                        S.add("pe", lambda e, k=k, vb=vb, sub=sub, t=t: e.matmul(
                            bank(vb), lhsT=h_sb[:, k, PAD + t * TW + sub * 128:PAD + t * TW + (sub + 1) * 128],
                            rhs=slotA[:, sVv, k, 0:512], start=(k == 0), stop=(k == 7)),
                            reads=[slotB[sVv], hB[k][t]], writes=[bankB[vb]])
                    gi_ = 2 if sub % 2 == 0 else 4
                    qi_ = sub % 2
                    S.add("act", lambda e, vb=vb, gi_=gi_: e.activation(out=T[gi_], in_=bank(vb), func=AF.Gelu_apprx_tanh),
                          reads=[bankB[vb]], writes=[TB[gi_]])
                    vi = ctr["vss"] % 2
                    ctr["vss"] += 1
                    ss, ssB_ = vss_sb[vi], vssB[vi]
                    sd1, sd1B = vss_sb[2 + vi], vssB[2 + vi]
                    S.add("dve", lambda e, gi_=gi_, qi_=qi_: e.tensor_tensor(out=T[qi_], in0=T[gi_], in1=T[gi_], op=ALU.mult),
                          reads=[TB[gi_]], writes=[TB[qi_]])
                    S.add("dve", lambda e, ss=ss, qi_=qi_: e.reduce_sum(out=ss, in_=T[qi_], axis=AX.X), reads=[TB[qi_]], writes=[ssB_])
                    S.add("act", lambda e, ss=ss, sd1=sd1: e.activation(out=sd1, in_=ss, func=AF.Sqrt, bias=EPS, scale=1.0 / 512),
                          reads=[ssB_], writes=[sd1B])
                    S.add("dve", lambda e, sd1=sd1: e.reciprocal(out=sd1, in_=sd1), reads=[sd1B], writes=[sd1B])
                    S.add("dve", lambda e, sd1=sd1, sub=sub, gi_=gi_: e.scalar_tensor_tensor(out=vn_sb[:, sub, :], in0=T[gi_], scalar=sd1, in1=sgug_sb,
                                                                                               op0=ALU.mult, op1=ALU.mult),
                          reads=[TB[gi_], sd1B, otabB], writes=[vnB[sub]])
                for g in range(4):
                    cb = 6 + (g % 2)
                    S.add("pe", lambda e, cb=cb, g=g: e.matmul(bank(cb), lhsT=poolw_sb[:, g, :], rhs=pooled_sb[g], start=True, stop=True),
                          reads=[otab2B, pooledB[g]], writes=[bankB[cb]])
                    S.add("act", lambda e, cb=cb, g=g: e.activation(out=cat_o[:, g, :], in_=bank(cb), func=AF.Copy, scale=pcol(PC_PSC + j * 4 + g)),
                          reads=[bankB[cb], parB], writes=[catB[g]])
                for g in range(4):
                    mb = 2 + g
                    for sub in range(4):
                        S.add("pe", lambda e, mb=mb, sub=sub, g=g: e.matmul(
                            psum[:, mb, sub * 128:(sub + 1) * 128], lhsT=vn_sb[:, sub, g * 128:(g + 1) * 128], rhs=wst_sb[:, g, :],
                            start=True, stop=True),
                            reads=[vnB[sub], otab2B], writes=[bankB[mb]])
                    mi = g % 2
                    bb = sgub_sb[:, g, :].unsqueeze(1).to_broadcast([128, 4, 128])
                    S.add("dve", lambda e, mb=mb, mi=mi, bb=bb: e.tensor_tensor(
                        out=T[mi].rearrange("p (a c) -> p a c", a=4), in0=bank(mb).rearrange("p (a c) -> p a c", a=4), in1=bb, op=ALU.add),
                        reads=[bankB[mb], otabB], writes=[TB[mi]])
                    S.add("dve", lambda e, g=g, mi=mi: e.tensor_tensor(out=cat_o[:, 4 + g, :], in0=T[mi], in1=gu_sb[:, g, :], op=ALU.mult),
                          reads=[TB[mi], guB[g]], writes=[catB[4 + g]])
                out_proj(cat_o, catB, t, (2, 3, 4, 5))

        slotUB = [Buf(f"slotU{i}") for i in range(3)]

        plist = []
        for s in range(nseq):
            for l in layers:
                for ph in phases:
                    if ph == "ffn1":
                        pf, bd = ffn_phase(l, 1)
                    elif ph == "ffn2":
                        pf, bd = ffn_phase(l, 2)
                    elif l % 2 == 0:
                        pf, bd = even_mixer(l)
                    else:
                        pf, bd = odd_mixer(l)
                    plist.append((s, "ffn" if ph.startswith("ffn") else "mix", pf, bd))
        for i, (s, kind, pf, bd) in enumerate(plist):
            first = (i == 0) or plist[i - 1][0] != s
            last = (i == len(plist) - 1) or plist[i + 1][0] != s
            if first:
                if i == 0:
                    pf()
                S.barrier()
                load_x(s)
                S.barrier()
            elif kind != plist[i - 1][1] or kind == "mix":
                S.barrier()
            bd()
            if i + 1 < len(plist):
                plist[i + 1][2]()
            if last:
                S.barrier()
                final_store(s)
        fin = S.add("sp", None, reads=[], writes=[])
        fin.deps = list(out_ops)

        block = stack.enter_context(nc.Block())
        S.emit(nc, stack, block)
    return nc, S


def _rope_tables():
    rows = SEQ // 64
    r_idx, c_idx = np.meshgrid(np.arange(rows), np.arange(64), indexing="ij")
    r_idx = r_idx.reshape(-1).astype(np.float32)
    c_idx = c_idx.reshape(-1).astype(np.float32)
    n_freq = 16
    inv = (np.float32(10000.0) ** (-np.arange(n_freq, dtype=np.float32) / np.float32(n_freq))).astype(np.float32)
    ang = np.concatenate([r_idx[:, None] * inv, c_idx[:, None] * inv], axis=-1).astype(np.float32)
    cos = np.cos(ang).astype(np.float32)
    sin = np.sin(ang).astype(np.float32)
    tab = np.zeros((128, 2, SEQ), np.float32)
    for p in range(128):
        d = p % 64
        i = d // 2
        tab[p, 0] = cos[:, i]
        tab[p, 1] = -sin[:, i] if d % 2 == 0 else sin[:, i]
    return tab


def _pool_edge():
    tab = np.zeros((128, 4, 16), np.float32)
    for g, w in enumerate((2, 4, 8, 16)):
        r = w // 2
        for i in range(16):
            t = i if i < 8 else SEQ - 16 + i
            lo = max(t - r, 0)
            hi = min(t + r, SEQ - 1)
            tab[:, g, i] = np.float32(1.0) / np.float32(hi - lo + 1)
    return tab


def prepare_shared(inp):
    f = lambda a: np.ascontiguousarray(np.asarray(a, dtype=np.float32))
    sh = {}
    for k in ("ffn1_w_in", "ffn1_w_out", "ffn2_w_in", "ffn2_w_out", "ev_w_out", "od_w_in", "od_w_out"):
        sh[k] = f(inp[k])
    ev = f(inp["ev_w_in"])
    q = ev[:, :, 1536:2048]
    k = ev[:, :, 2048:2176]
    v = ev[:, :, 2176:2304]
    swap = np.arange(512) ^ 1
    qs = q[:, :, swap]
    ks = k[:, :, np.arange(128) ^ 1]
    kd0 = np.concatenate([k[:, :, 0:64], k[:, :, 0:64]], -1)
    kd1 = np.concatenate([k[:, :, 64:128], k[:, :, 64:128]], -1)
    ksd0 = np.concatenate([ks[:, :, 0:64], ks[:, :, 0:64]], -1)
    ksd1 = np.concatenate([ks[:, :, 64:128], ks[:, :, 64:128]], -1)
    sh["ev_w_in"] = np.ascontiguousarray(np.concatenate([ev[:, :, 0:1536], q, qs, kd0, kd1, ksd0, ksd1, v], -1))
    assert sh["ev_w_in"].shape[-1] == EV_COLS
    sh["od_pool_w"] = np.ascontiguousarray(f(inp["od_pool_w"]).transpose(0, 2, 1, 3))
    sh["od_sgu_wt"] = np.ascontiguousarray(f(inp["od_sgu_w"]).transpose(0, 3, 1, 2))
    par = np.zeros((128, NPAR), np.float32)
    for l in range(4):
        for w, nm in enumerate(("ffn1_norm", "mix_norm", "ffn2_norm")):
            par[:, PC_NORM(l, w):PC_NORM(l, w) + 8] = f(inp[nm])[l].reshape(8, 128).T
    cw = f(inp["ev_conv_w"])
    qn = f(inp["ev_q_norm"])
    kn = f(inp["ev_k_norm"])
    idx = np.arange(128) % 64
    for j in range(2):
        for s_ in range(3):
            par[:, PC_CONV + j * 12 + s_ * 4:PC_CONV + j * 12 + s_ * 4 + 4] = cw[j, s_].reshape(4, 128).T
        par[:, PC_QK + j * 4 + 0] = qn[j][idx]
        par[:, PC_QK + j * 4 + 1] = qn[j][idx ^ 1]
        par[:, PC_QK + j * 4 + 2] = kn[j][idx]
        par[:, PC_QK + j * 4 + 3] = kn[j][idx ^ 1]
        par[:, PC_PSC + j * 4:PC_PSC + j * 4 + 4] = f(inp["od_pool_scale"])[j].reshape(4, 128).T
    sh["params"] = par
    sh["rope"] = _rope_tables()
    sh["fin_g"] = np.ascontiguousarray(np.broadcast_to(f(inp["final_norm"])[None, :], (128, D)))
    sh["sgu_g"] = np.ascontiguousarray(np.broadcast_to(f(inp["od_sgu_norm"])[:, None, :], (2, 128, 512)))
    sb_ = f(inp["od_sgu_b"])
    sh["sgu_b"] = np.ascontiguousarray(np.broadcast_to(sb_[:, None, :, :], (2, 128, 4, 128)))
    sh["pool_edge"] = _pool_edge()
    cst = np.zeros((128, 3, 128), np.float32)
    cst[:, 0, :] = np.eye(128, dtype=np.float32)
    cst[:, 1, :] = 1.0
    cst[:, 2, :] = (np.arange(128)[:, None] // 64 == np.arange(128)[None, :] // 64).astype(np.float32)
    sh["consts"] = cst
    return sh


_CACHE = {}


def kernel(**inputs):
    cfg = default_cfg()
    x = np.asarray(inputs["x"], dtype=np.float32)
    B = x.shape[0]
    per = B // N_CORES
    sh = prepare_shared(inputs)
    if "nc" not in _CACHE:
        _CACHE["nc"] = build_program(cfg)[0]
    nc = _CACHE["nc"]
    in_maps = []
    for c in range(N_CORES):
        m = dict(sh)
        m["x"] = np.ascontiguousarray(x[c * per:(c + 1) * per].reshape(per * SEQ, D))
        in_maps.append(m)
    res = run_bass_kernel_spmd(nc, in_maps, core_ids=list(range(N_CORES)))
    out = np.stack([np.asarray(r["out"]).reshape(per, SEQ, D) for r in res.results], 0).reshape(B, SEQ, D)
    return out.astype(np.float32)
```
